# Optimizing a Trainium2 kernel written in Bass

```python
import math
import jax, jax.numpy as jnp
from jax import lax
import numpy as np

D_MODEL = 1024
BATCH = 8
SEQ = 4096
DEPTH = 2

CHUNK = 64
MIX_WIDTH = D_MODEL
ATT_HEADS = 8
HEAD_DIM = 64
ATT_WIDTH = ATT_HEADS * HEAD_DIM
CONV_CH = MIX_WIDTH - ATT_WIDTH
CONV_WIDTH = 31
D_FF = 4 * D_MODEL
QBLK = 128
N_IN = 3 * ATT_WIDTH + ATT_HEADS + 2 * CONV_CH
EPS = 1e-6

kernel_name = "fox_conformer_hybrid_trunk"


def rms_norm(x, g):
    xf = x.astype(jnp.float32)
    y = xf * lax.rsqrt(jnp.mean(xf * xf, axis=-1, keepdims=True) + EPS)
    return (y * g.astype(jnp.float32)).astype(x.dtype)


def layer_norm(x, g, b):
    xf = x.astype(jnp.float32)
    mu = jnp.mean(xf, axis=-1, keepdims=True)
    var = jnp.mean(jnp.square(xf - mu), axis=-1, keepdims=True)
    y = (xf - mu) * lax.rsqrt(var + EPS)
    return (y * g.astype(jnp.float32) + b.astype(jnp.float32)).astype(x.dtype)


def forgetting_attention(q, k, v, logf):
    S = q.shape[2]
    scale = 1.0 / math.sqrt(q.shape[-1])
    c = jnp.cumsum(logf.astype(jnp.float32), axis=-1)
    outs = []
    for blk in range(S // QBLK):
        q0, q1 = blk * QBLK, (blk + 1) * QBLK
        qb = q[:, :, q0:q1]
        kb = k[:, :, :q1]
        vb = v[:, :, :q1]
        s = (jnp.einsum('bhqd,bhkd->bhqk', qb, kb).astype(jnp.float32) * scale
             + (c[:, :, q0:q1, None] - c[:, :, None, :q1]))
        mask = jnp.arange(q0, q1)[:, None] >= jnp.arange(q1)[None, :]
        s = jnp.where(mask, s, -jnp.inf)
        p = jax.nn.softmax(s, axis=-1)
        outs.append(jnp.einsum('bhqk,bhkd->bhqd', p.astype(vb.dtype), vb))
    return jnp.concatenate(outs, axis=2)


def causal_depthwise_conv(x, w, b):
    W, C = w.shape
    y = lax.conv_general_dilated(
        x, w.reshape(W, 1, C).astype(x.dtype), window_strides=(1,), padding=[(W - 1, 0)],
        dimension_numbers=('NWC', 'WIO', 'NWC'), feature_group_count=C)
    return y + b.astype(x.dtype)


def hybrid_mixer(x, norm_g, w_in, b_f, q_norm_g, k_norm_g, conv_w, conv_b, conv_ln_g, conv_ln_b, w_o):
    B, S, _ = x.shape
    u = rms_norm(x, norm_g)
    proj = jnp.einsum('bsd,dn->bsn', u, w_in)
    o1 = ATT_WIDTH; o2 = 2 * ATT_WIDTH; o3 = 3 * ATT_WIDTH; o4 = o3 + ATT_HEADS
    q = proj[..., :o1].reshape(B, S, ATT_HEADS, HEAD_DIM)
    k = proj[..., o1:o2].reshape(B, S, ATT_HEADS, HEAD_DIM)
    v = proj[..., o2:o3].reshape(B, S, ATT_HEADS, HEAD_DIM)
    f_logit = proj[..., o3:o4]
    glu_in = proj[..., o4:]

    q = rms_norm(q, q_norm_g).transpose(0, 2, 1, 3)
    k = rms_norm(k, k_norm_g).transpose(0, 2, 1, 3)
    v = v.transpose(0, 2, 1, 3)
    logf = jax.nn.log_sigmoid(f_logit.astype(jnp.float32) + b_f.astype(jnp.float32)).transpose(0, 2, 1)
    att = forgetting_attention(q, k, v, logf).transpose(0, 2, 1, 3).reshape(B, S, ATT_WIDTH)

    a, g = jnp.split(glu_in, 2, axis=-1)
    h = a * jax.nn.sigmoid(g)
    h = causal_depthwise_conv(h, conv_w, conv_b)
    h = layer_norm(h, conv_ln_g, conv_ln_b)
    h = jax.nn.silu(h)

    mixed = jnp.concatenate([att, h.astype(att.dtype)], axis=-1)
    return x + jnp.einsum('bsm,md->bsd', mixed, w_o)


def sq_relu_mlp(x, norm_g, w1, w2):
    u = rms_norm(x, norm_g)
    h = jnp.square(jax.nn.relu(jnp.einsum('bsd,df->bsf', u, w1)))
    return x + jnp.einsum('bsf,fd->bsd', h, w2)


def setup_inputs(seed: int = 0) -> dict:
    key = jax.random.key(seed)
    ks = jax.random.split(key, 16)
    f32 = jnp.float32
    nrm = lambda k, shape, s: jax.random.normal(k, shape, f32) * s
    return {
        "x": jax.random.normal(ks[0], (BATCH, SEQ, D_MODEL), f32),
        "norm1_g": 1.0 + nrm(ks[1], (DEPTH, D_MODEL), 0.02),
        "w_in": nrm(ks[2], (DEPTH, D_MODEL, N_IN), D_MODEL ** -0.5),
        "b_f": 3.0 + nrm(ks[3], (DEPTH, ATT_HEADS), 0.1),
        "q_norm_g": 1.0 + nrm(ks[4], (DEPTH, HEAD_DIM), 0.02),
        "k_norm_g": 1.0 + nrm(ks[5], (DEPTH, HEAD_DIM), 0.02),
        "conv_w": nrm(ks[6], (DEPTH, CONV_WIDTH, CONV_CH), CONV_WIDTH ** -0.5),
        "conv_b": nrm(ks[7], (DEPTH, CONV_CH), 0.01),
        "conv_ln_g": 1.0 + nrm(ks[8], (DEPTH, CONV_CH), 0.02),
        "conv_ln_b": nrm(ks[9], (DEPTH, CONV_CH), 0.01),
        "w_o": nrm(ks[10], (DEPTH, MIX_WIDTH, D_MODEL), MIX_WIDTH ** -0.5),
        "norm2_g": 1.0 + nrm(ks[11], (DEPTH, D_MODEL), 0.02),
        "w_mlp_in": nrm(ks[12], (DEPTH, D_MODEL, D_FF), D_MODEL ** -0.5),
        "w_mlp_out": nrm(ks[13], (DEPTH, D_FF, D_MODEL), D_FF ** -0.5),
    }


def reference(x, norm1_g, w_in, b_f, q_norm_g, k_norm_g, conv_w, conv_b, conv_ln_g, conv_ln_b,
              w_o, norm2_g, w_mlp_in, w_mlp_out):
    for l in range(DEPTH):
        x = hybrid_mixer(x, norm1_g[l], w_in[l], b_f[l], q_norm_g[l], k_norm_g[l], conv_w[l],
                         conv_b[l], conv_ln_g[l], conv_ln_b[l], w_o[l])
        x = sq_relu_mlp(x, norm2_g[l], w_mlp_in[l], w_mlp_out[l])
    return x
```

```python
import numpy as np
import concourse.bass as bass
import concourse.mybir as mybir
from concourse.bass_utils import run_bass_kernel_spmd

F32 = mybir.dt.float32
BF16 = mybir.dt.bfloat16
AF = mybir.ActivationFunctionType
ALU = mybir.AluOpType

SEQ = 4096
D = 1024
T = 512
NT = SEQ // T
NSLAB = 23
NSLOT = 3
EPS = 1e-6
NVL = 186
NEG = -30000.0

SL_Q, SL_K, SL_V, SL_A, SL_G, SL_O0, SL_W1, SL_W2 = 0, 1, 2, 3, 4, 5, 7, 15


class Op:
    __slots__ = ("eng", "fn", "cdeps", "ddeps", "idx", "sig", "dsem", "dcount", "sigidx", "group")


class Sched:
    ENGS = ("pe", "act", "dve", "pool", "sp")

    def __init__(self):
        self.q = {e: [] for e in self.ENGS}
        self.lw = {}
        self.rd = {}
        self.dcnt = {}

    def add(self, eng, fn, reads=(), writes=(), dsem=None, group=None):
        op = Op()
        op.eng = eng
        op.fn = fn
        op.sig = False
        op.dsem = dsem
        op.dcount = 0
        op.group = group
        deps = []
        for r in reads:
            w = self.lw.get(r)
            if w is not None:
                deps.append(w)
        for r in writes:
            w = self.lw.get(r)
            if w is not None and not (group is not None and w.group == group):
                deps.append(w)
            rr = self.rd.get(r)
            if rr:
                deps.extend(rr.values())
        cd = {}
        dd = {}
        for d_ in deps:
            if d_.dsem is not None:
                if dd.get(d_.dsem, 0) < d_.dcount:
                    dd[d_.dsem] = d_.dcount
            else:
                if d_.eng == "pe" and eng == "pe":
                    continue
                if cd.get(d_.eng, -1) < d_.idx:
                    cd[d_.eng] = d_.idx
                    d_.sig = True
        op.cdeps = cd
        op.ddeps = dd
        op.idx = len(self.q[eng])
        self.q[eng].append(op)
        if dsem is not None:
            self.dcnt[dsem] = self.dcnt.get(dsem, 0) + 16
            op.dcount = self.dcnt[dsem]
        for r in reads:
            if r in writes:
                continue
            m = self.rd.setdefault(r, {})
            key = eng if dsem is None else ("dma", id(op))
            m[key] = op
        for r in writes:
            self.lw[r] = op
            self.rd[r] = {}
        return op

    def emit(self, eng, e, esems, dsems):
        ops = self.q[eng]
        known = {}
        for op in ops:
            for pe_, idx in op.cdeps.items():
                tgt = self.q[pe_][idx]
                val = tgt.sigidx
                if known.get(pe_, 0) < val:
                    e.wait_ge(esems[pe_], val)
                    known[pe_] = val
            for ds, cnt in op.ddeps.items():
                if known.get(ds, 0) < cnt:
                    e.wait_ge(dsems[ds], cnt)
                    known[ds] = cnt
            ins = op.fn(e)
            if op.dsem is not None:
                ins.then_inc(dsems[op.dsem], 16)
            elif op.sig:
                ins.then_inc(esems[eng], 1)

    def finalize(self):
        for eng in self.ENGS:
            n = 0
            for op in self.q[eng]:
                if op.sig and op.dsem is None:
                    n += 1
                    op.sigidx = n
                else:
                    op.sigidx = None


def build_program(nl, in_tok, out_tok):
    nc = bass.Bass("TRN2", target_bir_lowering=False)
    if in_tok:
        x_in = nc.dram_tensor("x", [SEQ, D], F32, kind="ExternalInput").ap()
    else:
        x_in = nc.dram_tensor("x", [NT, 128, 8 * T], F32, kind="ExternalInput").ap()
    w_in = nc.dram_tensor("w_in", [nl, D, 2568], F32, kind="ExternalInput").ap()
    w_o = nc.dram_tensor("w_o", [nl, D, D], F32, kind="ExternalInput").ap()
    w1 = nc.dram_tensor("w1", [nl, D, 4 * D], F32, kind="ExternalInput").ap()
    w2 = nc.dram_tensor("w2", [nl, 4 * D, D], F32, kind="ExternalInput").ap()
    vecs_d = nc.dram_tensor("vecs", [128, nl * NVL], F32, kind="ExternalInput").ap()
    if out_tok:
        y_out = nc.dram_tensor("out", [SEQ, D], F32, kind="ExternalOutput").ap()
    else:
        y_out = nc.dram_tensor("out", [NT, 128, 8 * T], F32, kind="ExternalOutput").ap()
    wsc = nc.dram_tensor("wsc", [nl, NSLAB, 128, 4096], BF16, kind="Internal").ap()
    xmid = nc.dram_tensor("xmid", [max(nl - 1, 1), NT, 128, 8 * T], F32, kind="Internal").ap()

    S = Sched()
    from contextlib import ExitStack
    with ExitStack() as es:
        def sb(name, shape, dt):
            return es.enter_context(nc.sbuf_tensor(name, shape, dt))

        def sem(name):
            return es.enter_context(nc.semaphore(name))

        Kc = sb("Kc", [128, 4 * SEQ], BF16)
        Vc = sb("Vc", [128, 32 * 520], BF16)
        Call = sb("Call", [128, 32 * 8], F32)
        BT = sb("BT", [128, 32 * 8], F32)
        carry = sb("carry", [128, 8], F32)
        cmid = sb("cmid", [128, 8], F32)
        zf = sb("zf", [128, 32], F32)
        azf = sb("azf", [128, 32], F32)
        ef = sb("ef", [128, 32], F32)
        mzf = sb("mzf", [128, 32], F32)
        logf = sb("logf", [128, 32], F32)
        X2 = [sb("xA", [128, 8 * T], F32), sb("xB", [128, 8 * T], F32)]
        stg = [sb(f"stg{k}", [128, D], F32) for k in range(2)]
        uT = sb("uT", [128, 8 * T], BF16)
        slots = [sb(f"slot{k}", [128, 4096], BF16) for k in range(NSLOT)]
        qT = sb("qT", [128, 8 * T], BF16)
        sq2 = [sb(f"sq2_{k}", [128, T], BF16) for k in range(2)]
        hin = sb("hin", [128, 4 * 542], BF16)
        dg = [sb(f"dg{k}", [128, 128], BF16) for k in range(8)]
        TR = sb("TR", [128, 16384], BF16)
        TRf = TR.bitcast(F32)
        rs_t = [sb(f"rs{k}", [128, T], F32) for k in range(2)]
        Pb = [sb(f"P{k}", [128, T], BF16) for k in range(3)]
        Osb = [sb(f"Osb{k}", [128, T], F32) for k in range(2)]
        rinv = sb("rinv", [128, T], F32)
        vecs = sb("vecs_sb", [128, nl * NVL], F32)
        wf = sb("wf", [128, nl * 64], BF16)
        ident_f = sb("ident_f", [128, 128], F32)
        ident_b = sb("ident_b", [128, 128], BF16)
        negmask = sb("negmask", [128, 128], BF16)
        tri_f = sb("tri_f", [128, 128], F32)
        ones_f = sb("ones_f", [128, 128], F32)
        ones_b = sb("ones_b", [128, 128], BF16)
        blk_b = sb("blk_b", [128, 128], BF16)
        mean_f = sb("mean_f", [128, 128], F32)
        mean_b = sb("mean_b", [128, 128], BF16)
        sel_a = sb("sel_a", [128, 128], F32)
        sel_b = sb("sel_b", [128, 128], F32)
        epsc = sb("epsc", [128, 2], F32)
        ps = [es.enter_context(nc.psum_tensor(f"ps{k}", [128, T], F32)) for k in range(8)]

        def hT(fc):
            return TR[:, fc * T:(fc + 1) * T]

        def sig_v(cc):
            return TRf[:, cc * T:(cc + 1) * T]

        def acc_v(cc):
            return TRf[:, 2048 + cc * T: 2048 + (cc + 1) * T]

        def mix_v(m):
            return TR[:, 8192 + m * T: 8192 + (m + 1) * T]

        sqb = TR[:, 12288:16384]
        R_SIG = lambda cc: [("TR", 2 * cc), ("TR", 2 * cc + 1)]
        R_ACC = lambda cc: [("TR", 8 + 2 * cc), ("TR", 8 + 2 * cc + 1)]
        R_MIX = lambda m: [("TR", 16 + m)]
        R_SQB = [("TR", 24 + c) for c in range(8)]
        R_HT = lambda fc: [("TR", fc)]

        esems = {e: sem("e_" + e) for e in Sched.ENGS}
        dsem_names = ["vec", "wfd", "xin0", "xin1", "xout0", "xout1"] + [f"xf{c}" for c in range(8)] + [f"of{c}" for c in range(8)] + \
            [f"slot{k}" for k in range(NSLOT)] + [f"pp{l}_{g}" for l in range(nl) for g in range(10)]
        dsems = {n: sem("d_" + n) for n in dsem_names}

        def A(eng, method, reads, writes, *args, **kw):
            dsem = kw.pop("dsem", None)
            group = kw.pop("group", None)
            return S.add(eng, (lambda e: getattr(e, method)(*args, **kw)), reads, writes, dsem=dsem, group=group)

        def MM(out, lhsT, rhs, start, stop, reads, writes):
            return S.add("pe", (lambda e: e.matmul(out, lhsT, rhs, start=start, stop=stop)), reads, writes)

        ring = [[0, 1, 2, 3, 4, 5]]
        CONV_BANK = 6
        ring_pos = [0]

        def next_bank():
            b = ring[0][ring_pos[0] % len(ring[0])]
            ring_pos[0] += 1
            return b
        STAT = [[6, 7]]
        stat_pos = [0]

        def next_stat():
            b = STAT[0][stat_pos[0] % len(STAT[0])]
            stat_pos[0] += 1
            return b
        rs_pos = [0]
        dg_pos = [0]

        def PSr(b):
            return [("ps", b)]

        A("pool", "memset", [], ["ident_f"], ident_f[:], 0.0)
        A("pool", "affine_select", ["ident_f"], ["ident_f"], out=ident_f[:], in_=ident_f[:], pattern=[[-1, 128]],
          compare_op=ALU.not_equal, fill=1.0, base=0, channel_multiplier=1)
        A("pool", "tensor_copy", ["ident_f"], ["ident_b"], out=ident_b[:], in_=ident_f[:])
        A("pool", "memset", [], ["negmask"], negmask[:], 0.0)
        A("pool", "affine_select", ["negmask"], ["negmask"], out=negmask[:], in_=negmask[:], pattern=[[1, 128]],
          compare_op=ALU.is_ge, fill=NEG, base=0, channel_multiplier=-1)
        A("pool", "memset", [], ["tri_f"], tri_f[:], 1.0)
        A("pool", "affine_select", ["tri_f"], ["tri_f"], out=tri_f[:], in_=tri_f[:], pattern=[[1, 128]],
          compare_op=ALU.is_ge, fill=0.0, base=0, channel_multiplier=-1)
        A("pool", "memset", [], ["ones_f"], ones_f[:], 1.0)
        A("pool", "memset", [], ["ones_b"], ones_b[:], 1.0)
        A("pool", "memset", [], ["blk_b"], blk_b[:], 0.0)
        A("pool", "memset", ["blk_b"], ["blk_b"], blk_b[0:64, 0:64], 1.0)
        A("pool", "memset", ["blk_b"], ["blk_b"], blk_b[64:128, 64:128], 1.0)
        A("pool", "memset", [], ["mean_f"], mean_f[:], 1.0 / 512.0)
        A("pool", "memset", [], ["mean_b"], mean_b[:], 1.0 / 512.0)
        A("pool", "memset", [], ["sel_a"], sel_a[:], 0.0)
        A("pool", "memset", [], ["sel_b"], sel_b[:], 0.0)
        A("pool", "affine_select", ["sel_a"], ["sel_a"], out=sel_a[:, 0:64], in_=sel_a[:, 0:64], pattern=[[0, 64]],
          compare_op=ALU.not_equal, fill=1.0, base=-64, channel_multiplier=1)
        A("pool", "affine_select", ["sel_b"], ["sel_b"], out=sel_b[:, 64:128], in_=sel_b[:, 64:128], pattern=[[0, 64]],
          compare_op=ALU.not_equal, fill=1.0, base=-63, channel_multiplier=1)
        A("pool", "memset", [], ["epsc"], epsc[:, 0:1], EPS)
        A("pool", "memset", ["epsc"], ["epsc"], epsc[:, 1:2], 64.0 * EPS)
        A("pool", "memset", [], [("Osb", 0)], Osb[0][:], 0.0)
        A("pool", "memset", [], [("Osb", 1)], Osb[1][:], 0.0)
        A("pool", "memset", [], [("qT", h_) for h_ in range(8)], qT[:], 0.0)
        A("pool", "memset", [], [("Vones",)], bass.AP(Vc, 64, [[32 * 520, 128], [65, 256], [1, 1]]), 1.0)

        A("sp", "dma_start", [], ["vecs"], out=vecs[:], in_=vecs_d, dsem="vec")
        for lp in range(nl):
            A("pool", "dma_start", [], ["wf"], out=bass.AP(wf, lp * 64, [[nl * 64, 128], [8, 8], [1, 8]]),
              in_=w_in[lp, :, 1536:1544].rearrange("(c p) n -> p c n", p=128), dsem="wfd")

        def slab_parts(lp, s_):
            parts = []
            if s_ < 5:
                c0 = [0, 512, 1024, 1544, 2056][s_]
                parts.append((0, 4096, ("p (c n) -> p c n", dict(c=8)),
                              w_in[lp, :, c0:c0 + 512].rearrange("(c p) n -> p c n", p=128)))
            elif s_ < 7:
                h_ = s_ - SL_O0
                parts.append((0, 4096, ("p (c n) -> p c n", dict(c=8)),
                              w_o[lp, :, h_ * 512:(h_ + 1) * 512].rearrange("(c p) n -> p c n", p=128)))
            elif s_ < 15:
                f_ = s_ - SL_W1
                parts.append((0, 4096, ("p (c n) -> p c n", dict(c=8)),
                              w1[lp, :, f_ * 512:(f_ + 1) * 512].rearrange("(c p) n -> p c n", p=128)))
            else:
                dc = s_ - SL_W2
                for qd in range(4):
                    parts.append((qd * 1024, 1024, ("p (f n) -> p f n", dict(f=8)),
                                  w2[lp, qd * 1024:(qd + 1) * 1024, dc * 128:(dc + 1) * 128].rearrange("(f p) n -> p f n", p=128)))
            return parts

        GROUP_OF = lambda s_: s_ if s_ < 5 else (5 if s_ < 7 else (6 + (s_ - 7) // 4 if s_ < 15 else 8 + (s_ - 15) // 4))

        def prepass(lp):
            for s_ in range(NSLAB):
                for off, ln, (rs_, kw), src in slab_parts(lp, s_):
                    A("pool", "dma_start", [], [("wsc", lp, GROUP_OF(s_))],
                      out=wsc[lp, s_, :, off:off + ln].rearrange(rs_, **kw), in_=src, dsem=f"pp{lp}_{GROUP_OF(s_)}",
                      group=("pp", lp))

        slab_ctr = [0]
        preloaded = {}

        def preload_slab(lp, i, s_):
            k = slab_ctr[0] % NSLOT
            slab_ctr[0] += 1
            if lp == 0 and i == 0:
                for off, ln, (rs_, kw), src in slab_parts(0, s_):
                    A("pool", "dma_start", [], [("slot", k)], out=slots[k][:, off:off + ln].rearrange(rs_, **kw), in_=src,
                      dsem=f"slot{k}", group=("fill", slab_ctr[0]))
                A("sp", "dma_start", [("slot", k)], [("wsc", 0, GROUP_OF(s_))], out=wsc[0, s_], in_=slots[k][:],
                  dsem=f"pp0_{GROUP_OF(s_)}")
            else:
                A("sp", "dma_start", [("wsc", lp, GROUP_OF(s_))], [("slot", k)], out=slots[k][:], in_=wsc[lp, s_],
                  dsem=f"slot{k}")
            preloaded[(lp, i, s_)] = k

        def load_slab(lp, s_):
            key = (lp, cur_tile[0], s_)
            if key not in preloaded:
                preload_slab(lp, cur_tile[0], s_)
            return preloaded.pop(key)

        cur_tile = [0]

        def V(lp, off, n):
            return vecs[:, lp * NVL + off: lp * NVL + off + n]

        def rms_sq_chunk(c, b, xt, XK):
            A("act", "activation", [XK(c)], [("TR", 24 + c)], out=sqb[:, c * T:(c + 1) * T], in_=xt[:, c * T:(c + 1) * T],
              func=AF.Square)
            MM(ps[b][:], ones_b[:], sqb[:, c * T:(c + 1) * T], c == 0, c == 7, [("TR", 24 + c), "ones_b"], PSr(b))

        def rms_finish(lp, goff, b, xt, XK):
            k = rs_pos[0] % 2
            rs_pos[0] += 1
            A("act", "activation", PSr(b) + ["epsc"], [("rs", k)], out=rs_t[k][:], in_=ps[b][:], func=AF.Ln,
              bias=epsc[:, 0:1], scale=1.0 / D)
            A("act", "activation", [("rs", k)], [("rs", k)], out=rs_t[k][:], in_=rs_t[k][:], func=AF.Exp, scale=-0.5)
            for c in range(8):
                A("dve", "scalar_tensor_tensor", [XK(c), ("rs", k), "vecs"], [("uT", c)],
                  out=uT[:, c * T:(c + 1) * T], in0=xt[:, c * T:(c + 1) * T], scalar=V(lp, goff + c, 1),
                  in1=rs_t[k][:], op0=ALU.mult, op1=ALU.mult)

        def prefetch_dma(lp, i, par):
            xt = X2[par]
            XK = lambda c: ("xT", par, c)
            if lp == 0 and in_tok:
                for r in range(2):
                    A("sp", "dma_start", [], [("stg", r)], out=stg[r][:], in_=x_in[i * T + r * 128: i * T + (r + 1) * 128, :],
                      dsem=f"xin{r}")
            elif lp == 0:
                for c in range(8):
                    A("sp", "dma_start", [], [XK(c)], out=xt[:, c * T:(c + 1) * T], in_=x_in[i, :, c * T:(c + 1) * T],
                      dsem=f"xf{c}")
            else:
                for c in range(8):
                    A("sp", "dma_start", [("xmid", lp - 1, i, c)], [XK(c)], out=xt[:, c * T:(c + 1) * T],
                      in_=xmid[lp - 1, i, :, c * T:(c + 1) * T], dsem=f"xf{c}")

        def prefetch_stages(lp, i, par):
            xt = X2[par]
            XK = lambda c: ("xT", par, c)
            stages = []
            if lp == 0 and in_tok:
                def tr_stage(r):
                    k = r % 2
                    for hf in range(2):
                        b = next_bank()
                        for cq in range(4):
                            c = hf * 4 + cq
                            S.add("pe", (lambda e, b=b, cq=cq, c=c, k=k: e.transpose(
                                ps[b][:, cq * 128:(cq + 1) * 128], stg[k][:, c * 128:(c + 1) * 128], ident_f[:])),
                                [("stg", k), "ident_f"], PSr(b))
                        A("dve", "tensor_copy", PSr(b), [XK(hf * 4 + cq) for cq in range(4)],
                          out=bass.AP(xt, hf * 4 * T + r * 128, [[8 * T, 128], [T, 4], [1, 128]]),
                          in_=bass.AP(ps[b], 0, [[T, 128], [128, 4], [1, 128]]))
                    if r + 2 < 4:
                        A("act", "dma_start", [], [("stg", k)], out=stg[k][:],
                          in_=x_in[i * T + (r + 2) * 128: i * T + (r + 3) * 128, :], dsem=f"xin{k}")
                for r in range(4):
                    stages.append(lambda r=r: tr_stage(r))
            sb_ = {}

            def sq_pair(c0):
                for c in (c0, c0 + 1):
                    A("act", "activation", [XK(c)], [("sq2", c % 2)], out=sq2[c % 2][:], in_=xt[:, c * T:(c + 1) * T],
                      func=AF.Square)

            def mm_pair(c0):
                if c0 == 0:
                    sb_["b"] = next_stat()
                b = sb_["b"]
                for c in (c0, c0 + 1):
                    MM(ps[b][:], ones_b[:], sq2[c % 2][:], c == 0, c == 7, [("sq2", c % 2), "ones_b"], PSr(b))

            def st_a():
                sq_pair(0)

            def st_mid(c0):
                mm_pair(c0)
                sq_pair(c0 + 2)

            def st_end():
                mm_pair(6)
                rms_finish(lp, 0, sb_["b"], xt, XK)
            if stages:
                last_tr = stages.pop()
                stages.append(lambda: (last_tr(), st_a()))
            else:
                stages.append(st_a)
            stages.extend([lambda: st_mid(0), lambda: st_mid(2), lambda: st_mid(4), st_end])
            return stages

        U_ALL = [("uT", c) for c in range(8)]

        def proj_chunk(slot, col0):
            b = next_bank()
            for c in range(8):
                MM(ps[b][:], slots[slot][:, c * 512 + col0: c * 512 + col0 + 128], uT[:, c * T:(c + 1) * T],
                   c == 0, c == 7, [("slot", slot), ("uT", c)], PSr(b))
            return b

        pending_store = []
        prefetch_dma(0, 0, 0)
        for st_ in prefetch_stages(0, 0, 0):
            st_()
        for lp in range(nl):
            A("dve", "memset", [], ["carry"], carry[:], 0.0)
            A("dve", "memset", [], [("hin", cc) for cc in range(4)], hin[:], 0.0)
            last_layer = (lp == nl - 1)
            for i in range(NT):
                t0 = i * T
                cur_tile[0] = i
                par = (lp * NT + i) % 2
                xT = X2[par]
                XK = (lambda par: (lambda c: ("xT", par, c)))(par)
                if i + 1 < NT:
                    nxt = (lp, i + 1)
                elif lp + 1 < nl:
                    nxt = (lp + 1, 0)
                else:
                    nxt = None
                if i == 1 and lp + 1 < nl:
                    prepass(lp + 1)

                def qk_post(kind, p, b, k2):
                    sbk = next_stat()
                    MM(ps[sbk][:], blk_b[:], sq2[k2][:], True, True, [("sq2", k2), "blk_b"], PSr(sbk))
                    k = rs_pos[0] % 2
                    rs_pos[0] += 1
                    if kind == 0:
                        A("act", "activation", PSr(sbk) + ["epsc"], [("rs", k)], out=rs_t[k][:], in_=ps[sbk][:],
                          func=AF.Ln, bias=epsc[:, 1:2], scale=1.0)
                    else:
                        A("act", "activation", PSr(sbk) + ["epsc"], [("rs", k)], out=rs_t[k][:], in_=ps[sbk][:],
                          func=AF.Ln, bias=epsc[:, 0:1], scale=1.0 / 64.0)
                    A("act", "activation", [("rs", k)], [("rs", k)], out=rs_t[k][:], in_=rs_t[k][:], func=AF.Exp, scale=-0.5)
                    if kind == 0:
                        for e_ in range(2):
                            pr = slice(e_ * 64, (e_ + 1) * 64)
                            h_ = 2 * p + e_
                            A("dve", "scalar_tensor_tensor", PSr(b) + [("rs", k), "vecs"], [("qT", h_)],
                              out=qT[pr, h_ * T:(h_ + 1) * T], in0=ps[b][pr, :], scalar=vecs[pr, lp * NVL + 16: lp * NVL + 17],
                              in1=rs_t[k][pr, :], op0=ALU.mult, op1=ALU.mult)
                    else:
                        A("dve", "scalar_tensor_tensor", PSr(b) + [("rs", k), "vecs"], [("Kc", p, i)],
                          out=Kc[:, p * SEQ + t0: p * SEQ + t0 + T], in0=ps[b][:], scalar=V(lp, 17, 1),
                          in1=rs_t[k][:], op0=ALU.mult, op1=ALU.mult)

                pend = None
                qk_slots = [load_slab(lp, SL_Q), load_slab(lp, SL_K)]
                for kind in range(2):
                    for p in range(4):
                        b = proj_chunk(qk_slots[kind], p * 128)
                        k2 = (kind * 4 + p) % 2
                        A("act", "activation", PSr(b), [("sq2", k2)], out=sq2[k2][:], in_=ps[b][:], func=AF.Square)
                        if pend is not None:
                            qk_post(*pend)
                        pend = (kind, p, b, k2)
                qk_post(*pend)

                slot = load_slab(lp, SL_V)
                fb = next_stat()
                for r in range(4):
                    b = next_bank()
                    for c in range(8):
                        MM(ps[b][:], uT[:, c * T + r * 128: c * T + (r + 1) * 128], slots[slot][:, c * 512:(c + 1) * 512],
                           c == 0, c == 7, [("slot", slot), ("uT", c)], PSr(b))
                    blk = i * 4 + r
                    A("dve", "tensor_copy", PSr(b), [("Vc", i, r)],
                      out=bass.AP(Vc, blk * 520, [[32 * 520, 128], [65, 8], [1, 64]]),
                      in_=bass.AP(ps[b], 0, [[T, 128], [64, 8], [1, 64]]))
                    for c in range(8):
                        MM(ps[fb][:, r * 8:(r + 1) * 8], uT[:, c * T + r * 128: c * T + (r + 1) * 128],
                           wf[:, lp * 64 + c * 8: lp * 64 + (c + 1) * 8], c == 0, c == 7, [("uT", c), "wf"], PSr(fb))
                A("dve", "tensor_tensor", PSr(fb) + ["vecs"], ["zf"], out=zf[:], in0=ps[fb][:, 0:32], in1=V(lp, 154, 32),
                  op=ALU.add)
                A("dve", "scalar_tensor_tensor", ["zf"], ["azf"], out=azf[:], in0=zf[:], scalar=-1.0, in1=zf[:],
                  op0=ALU.mult, op1=ALU.max)
                A("act", "activation", ["azf"], ["ef"], out=ef[:], in_=azf[:], func=AF.Exp, scale=-1.0)
                A("act", "activation", ["ef"], ["ef"], out=ef[:], in_=ef[:], func=AF.Ln, bias=1.0, scale=1.0)
                A("dve", "tensor_scalar", ["zf"], ["mzf"], out=mzf[:], in0=zf[:], scalar1=0.0, scalar2=None, op0=ALU.min)
                A("dve", "tensor_tensor", ["mzf", "ef"], ["logf"], out=logf[:], in0=mzf[:], in1=ef[:], op=ALU.subtract)
                slot_a = load_slab(lp, SL_A)
                slot_g = load_slab(lp, SL_G)
                for cc in range(4):
                    ba = proj_chunk(slot_a, cc * 128)
                    bg = proj_chunk(slot_g, cc * 128)
                    A("act", "activation", PSr(bg), R_SIG(cc), out=sig_v(cc), in_=ps[bg][:], func=AF.Sigmoid)
                    A("dve", "tensor_tensor", PSr(ba) + R_SIG(cc), [("hin", cc)], out=hin[:, cc * 542 + 30: cc * 542 + 542],
                      in0=ps[ba][:], in1=sig_v(cc), op=ALU.mult)

                cb = next_stat()
                for r in range(4):
                    MM(ps[cb][:, r * 8:(r + 1) * 8], tri_f[:], logf[:, r * 8:(r + 1) * 8], True, r == 0,
                       ["logf", "tri_f"], PSr(cb))
                    for r2 in range(r):
                        MM(ps[cb][:, r * 8:(r + 1) * 8], ones_f[:], logf[:, r2 * 8:(r2 + 1) * 8], False, r2 == r - 1,
                           ["logf", "ones_f"], PSr(cb))
                for r in range(4):
                    MM(ps[cb][:, 32:40], ones_f[:], logf[:, r * 8:(r + 1) * 8], r == 0, r == 3, ["logf", "ones_f"], PSr(cb))
                for r in range(2):
                    MM(ps[cb][:, 40:48], ones_f[:], logf[:, r * 8:(r + 1) * 8], r == 0, r == 1, ["logf", "ones_f"], PSr(cb))
                A("dve", "tensor_tensor", PSr(cb) + ["carry"], [("Call", i)],
                  out=bass.AP(Call, i * 32, [[256, 128], [8, 4], [1, 8]]),
                  in0=bass.AP(ps[cb], 0, [[T, 128], [8, 4], [1, 8]]),
                  in1=bass.AP(carry, 0, [[8, 128], [0, 4], [1, 8]]), op=ALU.add)
                A("dve", "tensor_tensor", PSr(cb) + ["carry"], ["cmid"], out=cmid[:], in0=ps[cb][:, 40:48], in1=carry[:],
                  op=ALU.add)
                A("dve", "tensor_tensor", PSr(cb) + ["carry"], ["carry"], out=carry[:], in0=ps[cb][:, 32:40], in1=carry[:],
                  op=ALU.add)
                nj = 4 * i + 4
                A("dve", "tensor_tensor", ["cmid"] + [("Call", t) for t in range(i + 1)], ["BT"],
                  out=bass.AP(BT, 0, [[256, 128], [8, nj], [1, 8]]),
                  in0=bass.AP(cmid, 0, [[8, 128], [0, nj], [1, 8]]),
                  in1=bass.AP(Call, 0, [[256, 128], [8, nj], [1, 8]]), op=ALU.subtract)

                conv_thunks = []
                taps = [(cc, w_) for cc in range(4) for w_ in range(31)]
                tap_dg = {}
                built = [0]

                def build_upto(m):
                    while built[0] < min(m, len(taps)):
                        cc, w_ = taps[built[0]]
                        kd = dg_pos[0] % 8
                        dg_pos[0] += 1
                        tap_dg[built[0]] = kd
                        A("dve", "tensor_scalar", ["ident_b", "vecs"], [("dg", kd)], out=dg[kd][:], in0=ident_b[:],
                          scalar1=V(lp, 18 + cc * 31 + w_, 1), scalar2=None, op0=ALU.mult)
                        built[0] += 1

                def mk_conv(cc):
                    base = cc * 542

                    def tap(w_):
                        idx = cc * 31 + w_
                        build_upto(idx + 7)
                        kd = tap_dg[idx]
                        MM(ps[CONV_BANK][:], dg[kd][:], hin[:, base + w_: base + w_ + T], w_ == 0, w_ == 30,
                           [("dg", kd), ("hin", cc)], PSr(CONV_BANK))
                    for w_ in range(31):
                        conv_thunks.append(lambda w_=w_: tap(w_))

                    def fin():
                        A("dve", "tensor_scalar", PSr(CONV_BANK) + ["vecs"], R_ACC(cc), out=acc_v(cc), in0=ps[CONV_BANK][:],
                          scalar1=V(lp, 142 + cc, 1), scalar2=None, op0=ALU.add)
                        A("dve", "tensor_copy", [("hin", cc)], [("hin", cc)], out=hin[:, base: base + 30],
                          in_=hin[:, base + T: base + T + 30])
                        return 9
                    conv_thunks.append(fin)
                for cc in range(4):
                    mk_conv(cc)
                build_upto(7)

                sqd = TR[:, 0:4 * T]
                R_SQD = [("TR", k_) for k_ in range(4)]
                ln_banks = {}

                def ln_mean():
                    mb = CONV_BANK
                    ln_banks["m"] = mb
                    for cc in range(4):
                        MM(ps[mb][:], mean_f[:], acc_v(cc), cc == 0, cc == 3, R_ACC(cc) + ["mean_f"], PSr(mb))
                    return 4

                def ln_center():
                    mb = ln_banks["m"]
                    for cc in range(4):
                        A("dve", "tensor_tensor", PSr(mb) + R_ACC(cc), R_ACC(cc), out=acc_v(cc), in0=acc_v(cc), in1=ps[mb][:],
                          op=ALU.subtract)
                        A("dve", "tensor_tensor", R_ACC(cc), [("TR", cc)], out=sqd[:, cc * T:(cc + 1) * T], in0=acc_v(cc),
                          in1=acc_v(cc), op=ALU.mult)
                    return 18

                def ln_var():
                    vb = CONV_BANK
                    ln_banks["v"] = vb
                    for cc in range(4):
                        MM(ps[vb][:], mean_b[:], sqd[:, cc * T:(cc + 1) * T], cc == 0, cc == 3, [("TR", cc), "mean_b"], PSr(vb))
                    k = rs_pos[0] % 2
                    rs_pos[0] += 1
                    ln_banks["k"] = k
                    A("act", "activation", PSr(vb) + ["epsc"], [("rs", k)], out=rs_t[k][:], in_=ps[vb][:], func=AF.Ln,
                      bias=epsc[:, 0:1], scale=1.0)
                    A("act", "activation", [("rs", k)], [("rs", k)], out=rs_t[k][:], in_=rs_t[k][:], func=AF.Exp, scale=-0.5)
                    for cc in range(4):
                        A("dve", "tensor_tensor", R_ACC(cc) + [("rs", k)], R_ACC(cc), out=acc_v(cc), in0=acc_v(cc),
                          in1=rs_t[k][:], op=ALU.mult)
                    return None

                def ln_finish():
                    for cc in range(4):
                        A("act", "activation", R_ACC(cc) + ["vecs"], R_MIX(4 + cc), out=mix_v(4 + cc), in_=acc_v(cc),
                          func=AF.Silu, bias=V(lp, 150 + cc, 1), scale=V(lp, 146 + cc, 1))
                conv_thunks.extend([ln_mean, ln_center, ln_var])

                steps = []
                for p in range(4):
                    for e_ in range(2):
                        for j in range(nj):
                            c0 = 0 if j < 4 * i else (j - 4 * i) * 128
                            steps.append((p, e_, j, c0))
                LAG = 2
                ring[0] = [0, 1, 2]
                STAT[0] = [7]
                sbank = {}
                nsteps = len(steps)

                def emit_S(n):
                    p, e_, j, c0 = steps[n]
                    h = 2 * p + e_
                    b = next_bank()
                    sbank[n] = b
                    diag = j >= 4 * i
                    MM(ps[b][:, c0:T], Kc[:, p * SEQ + j * 128: p * SEQ + (j + 1) * 128], qT[:, h * T + c0: (h + 1) * T],
                       True, not diag, [("Kc", p, j // 4), ("qT", h)], PSr(b))
                    if diag:
                        MM(ps[b][:, c0:c0 + 128], ident_b[:], negmask[:], False, True, ["ident_b", "negmask"], PSr(b))
                    A("act", "activation", PSr(b) + ["BT"], [("P", n % 3)], out=Pb[n % 3][:, c0:T], in_=ps[b][:, c0:T],
                      func=AF.Exp, bias=BT[:, j * 8 + h: j * 8 + h + 1], scale=1.0)

                def emit_PV(n):
                    p, e_, j, c0 = steps[n]
                    h = 2 * p + e_
                    ob = 3 + h % 3
                    if e_ == 0:
                        lhsT = Vc[:, j * 520 + h * 65: j * 520 + h * 65 + 128]
                        out = ps[ob][:, c0:T]
                    else:
                        lhsT = Vc[:, j * 520 + (h - 1) * 65 + 1: j * 520 + (h - 1) * 65 + 129]
                        out = ps[ob][:, c0:T]
                    MM(out, lhsT, Pb[n % 3][:, c0:T], j == 0, j == nj - 1, [("P", n % 3), ("Vc", j // 4, j % 4), ("Vones",)],
                       PSr(ob))
                    if e_ == 1 and j == nj - 1:
                        emit_norm(p, n)

                deferred = []
                for q_, st_ in enumerate(pending_store):
                    deferred.append((3 + 3 * q_, st_))
                del pending_store[:]

                def emit_norm(p, n):
                    oa, ob_ = 3 + (2 * p) % 3, 3 + (2 * p + 1) % 3
                    A("dve", "tensor_copy", PSr(oa), [("Osb", 0)], out=Osb[0][0:65, :], in_=ps[oa][0:65, :])
                    A("dve", "tensor_copy", PSr(ob_), [("Osb", 1)], out=Osb[1][:], in_=ps[ob_][:])

                    def part2():
                        sbk = next_stat()
                        MM(ps[sbk][:], sel_a[:], Osb[0][:], True, False, [("Osb", 0), "sel_a"], PSr(sbk))
                        MM(ps[sbk][:], sel_b[:], Osb[1][:], False, True, [("Osb", 1), "sel_b"], PSr(sbk))
                        A("dve", "reciprocal", PSr(sbk), ["rinv"], out=rinv[:], in_=ps[sbk][:])
                        A("dve", "tensor_tensor", [("Osb", 0), "rinv"], R_MIX(p), out=mix_v(p)[0:64, :], in0=Osb[0][0:64, :],
                          in1=rinv[0:64, :], op=ALU.mult)
                        A("dve", "tensor_tensor", [("Osb", 1), "rinv"], R_MIX(p), out=mix_v(p)[64:128, :],
                          in0=Osb[1][64:128, :], in1=rinv[64:128, :], op=ALU.mult)
                    deferred.append((n + min(10, 2 * nj - 2), part2))

                per = -(-len(conv_thunks) // max(nsteps - 56, 8))
                hold = [0]
                for n in range(nsteps + LAG):
                    if n < nsteps:
                        emit_S(n)
                    if n - LAG >= 0:
                        emit_PV(n - LAG)
                    for d_ in [d_ for d_ in deferred if d_[0] <= n]:
                        deferred.remove(d_)
                        d_[1]()
                    for _ in range(per):
                        if conv_thunks and n >= hold[0]:
                            th = conv_thunks.pop(0)
                            r_ = th()
                            if r_:
                                hold[0] = n + r_
                                break
                while deferred:
                    deferred.pop(0)[1]()
                while conv_thunks:
                    conv_thunks.pop(0)()
                ln_finish()
                ring[0] = [0, 1, 2, 3, 4, 5]
                STAT[0] = [6, 7]

                n2b = next_stat()
                wo_slots = [load_slab(lp, SL_O0), load_slab(lp, SL_O0 + 1)]
                M_ORDER = [0, 1, 2, 4, 5, 6, 7, 3]

                def wo_mm(dc, b, m, first, last):
                    h_, dq = dc // 4, dc % 4
                    MM(ps[b][:], slots[wo_slots[h_]][:, m * 512 + dq * 128: m * 512 + (dq + 1) * 128], mix_v(m),
                       first, last, [("slot", wo_slots[h_])] + R_MIX(m), PSr(b))

                def wo_fin(dc, b):
                    A("dve", "tensor_tensor", PSr(b) + [XK(dc)], [XK(dc)], out=xT[:, dc * T:(dc + 1) * T],
                      in0=xT[:, dc * T:(dc + 1) * T], in1=ps[b][:], op=ALU.add)
                    A("dve", "tensor_scalar", [XK(dc), "vecs"], [("uT", dc)], out=uT[:, dc * T:(dc + 1) * T],
                      in0=xT[:, dc * T:(dc + 1) * T], scalar1=V(lp, 8 + dc, 1), scalar2=None, op0=ALU.mult)
                    A("act", "activation", [XK(dc)], [("TR", 24 + dc)], out=sqb[:, dc * T:(dc + 1) * T],
                      in_=xT[:, dc * T:(dc + 1) * T], func=AF.Square)

                wo_banks = {}
                for dc in range(6):
                    wo_banks[dc] = next_bank()
                    for mi, m in enumerate(M_ORDER[:7]):
                        wo_mm(dc, wo_banks[dc], m, mi == 0, False)
                for dc in range(6):
                    wo_mm(dc, wo_banks[dc], 3, False, True)
                for dc in range(6):
                    wo_fin(dc, wo_banks[dc])
                for dc in range(6, 8):
                    b = next_bank()
                    for mi, m in enumerate(M_ORDER):
                        wo_mm(dc, b, m, mi == 0, mi == 7)
                    wo_fin(dc, b)

                k2_ = rs_pos[0] % 2
                rs_pos[0] += 1

                def norm2_stats():
                    for c in range(8):
                        MM(ps[n2b][:], ones_b[:], sqb[:, c * T:(c + 1) * T], c == 0, c == 7, [("TR", 24 + c), "ones_b"],
                           PSr(n2b))
                    A("act", "activation", PSr(n2b) + ["epsc"], [("rs", k2_)], out=rs_t[k2_][:], in_=ps[n2b][:], func=AF.Ln,
                      bias=epsc[:, 0:1], scale=1.0 / D)
                    A("act", "activation", [("rs", k2_)], [("rs", k2_)], out=rs_t[k2_][:], in_=rs_t[k2_][:], func=AF.Exp,
                      scale=-1.0)
                if nxt is not None:
                    prefetch_dma(nxt[0], nxt[1], 1 - par)
                for s_ in range(8):
                    slot = load_slab(lp, SL_W1 + s_)
                    if s_ == 0:
                        w1b = [next_bank() for _ in range(4)]
                        for fq in range(4):
                            for c in range(7):
                                MM(ps[w1b[fq]][:], slots[slot][:, c * 512 + fq * 128: c * 512 + (fq + 1) * 128],
                                   uT[:, c * T:(c + 1) * T], c == 0, False, [("slot", slot), ("uT", c)], PSr(w1b[fq]))
                        for fq in range(4):
                            MM(ps[w1b[fq]][:], slots[slot][:, 7 * 512 + fq * 128: 7 * 512 + (fq + 1) * 128],
                               uT[:, 7 * T:8 * T], False, True, [("slot", slot), ("uT", 7)], PSr(w1b[fq]))
                    for fq in range(4):
                        fc = s_ * 4 + fq
                        b = w1b[fq] if s_ == 0 else proj_chunk(slot, fq * 128)
                        A("act", "activation", PSr(b), R_HT(fc), out=hT(fc), in_=ps[b][:], func=AF.Relu)
                        A("dve", "tensor_tensor", R_HT(fc), R_HT(fc), out=hT(fc), in0=hT(fc), in1=hT(fc), op=ALU.mult)
                    if s_ == 0:
                        norm2_stats()
                pf = prefetch_stages(nxt[0], nxt[1], 1 - par) if nxt is not None else []
                for dc in range(8):
                    if pf:
                        pf.pop(0)()
                    slot = load_slab(lp, SL_W2 + dc)
                    b = next_bank()
                    for fc in range(32):
                        MM(ps[b][:], slots[slot][:, fc * 128:(fc + 1) * 128], hT(fc), fc == 0, fc == 31,
                           [("slot", slot)] + R_HT(fc), PSr(b))
                    A("dve", "tensor_tensor", PSr(b) + [("rs", k2_)], ["rinv"], out=rinv[:], in0=ps[b][:], in1=rs_t[k2_][:],
                      op=ALU.mult)
                    A("dve", "tensor_tensor", ["rinv", XK(dc)], [XK(dc)], out=xT[:, dc * T:(dc + 1) * T],
                      in0=xT[:, dc * T:(dc + 1) * T], in1=rinv[:], op=ALU.add)
                    if not last_layer:
                        A("act", "dma_start", [XK(dc)], [("xmid", lp, i, dc)], out=xmid[lp, i, :, dc * T:(dc + 1) * T],
                          in_=xT[:, dc * T:(dc + 1) * T], dsem=f"of{dc}")
                    elif not out_tok:
                        A("act", "dma_start", [XK(dc)], [("out", i, dc)], out=y_out[i, :, dc * T:(dc + 1) * T],
                          in_=xT[:, dc * T:(dc + 1) * T], dsem=f"of{dc}")

                if nxt is not None:
                    preload_slab(nxt[0], nxt[1], SL_Q)
                    preload_slab(nxt[0], nxt[1], SL_K)
                if last_layer and out_tok:
                    def store_half(r, hf, i=i, t0=t0, xT=xT, XK=XK):
                        k = r % 2
                        b = next_stat()
                        for cq in range(4):
                            c = hf * 4 + cq
                            S.add("pe", (lambda e, b=b, cq=cq, c=c, r=r, xT=xT: e.transpose(
                                ps[b][:, cq * 128:(cq + 1) * 128], xT[:, c * T + r * 128: c * T + (r + 1) * 128],
                                ident_f[:])), [XK(c), "ident_f"], PSr(b))
                        A("dve", "tensor_copy", PSr(b), [("stg", k)], out=stg[k][:, hf * 512:(hf + 1) * 512], in_=ps[b][:])
                        if hf == 1:
                            A("pool", "dma_start", [("stg", k)], [("out", i, r)],
                              out=y_out[t0 + r * 128: t0 + (r + 1) * 128, :], in_=stg[k][:], dsem=f"xout{k}")
                    for r in range(4):
                        for hf in range(2):
                            pending_store.append(lambda r=r, hf=hf, f=store_half: f(r, hf))

        while pending_store:
            pending_store.pop(0)()

        outs = [r for r in S.lw if isinstance(r, tuple) and r[0] == "out"]
        A("sp", "nop", outs, [])

        S.finalize()
        with nc.Block() as block:
            @block.tensor
            def _(e):
                S.emit("pe", e, esems, dsems)

            @block.scalar
            def _(e):
                S.emit("act", e, esems, dsems)

            @block.vector
            def _(e):
                S.emit("dve", e, esems, dsems)

            @block.gpsimd
            def _(e):
                S.emit("pool", e, esems, dsems)

            @block.sync
            def _(e):
                S.emit("sp", e, esems, dsems)
    return nc


def pack_vecs(inp, layers):
    cols = []
    for l in layers:
        g1 = inp["norm1_g"][l].reshape(8, 128).T
        g2 = inp["norm2_g"][l].reshape(8, 128).T
        gq = np.tile(inp["q_norm_g"][l], 2)[:, None]
        gk = np.tile(inp["k_norm_g"][l], 2)[:, None]
        cw = inp["conv_w"][l].reshape(31, 4, 128).transpose(2, 1, 0).reshape(128, 124)
        cb = inp["conv_b"][l].reshape(4, 128).T
        lg = inp["conv_ln_g"][l].reshape(4, 128).T
        lb = inp["conv_ln_b"][l].reshape(4, 128).T
        bf = np.broadcast_to(np.tile(inp["b_f"][l], 4)[None, :], (128, 32))
        cols += [g1, g2, gq, gk, cw, cb, lg, lb, bf]
    return np.ascontiguousarray(np.concatenate(cols, axis=1).astype(np.float32))


_PROG_CACHE = {}


def _prog(nl, in_tok, out_tok):
    key = (nl, in_tok, out_tok)
    if key not in _PROG_CACHE:
        _PROG_CACHE[key] = build_program(nl, in_tok, out_tok)
    return _PROG_CACHE[key]


FUSED = True


def kernel(**inputs):
    inp = {k: np.asarray(v) for k, v in inputs.items()}
    x = np.ascontiguousarray(inp["x"], dtype=np.float32)
    ncores = 8
    depth = inp["w_in"].shape[0]
    if FUSED:
        stages = [list(range(depth))]
    else:
        stages = [[l] for l in range(depth)]
    cur = [x[b] for b in range(ncores)]
    for si, layers in enumerate(stages):
        in_tok = (si == 0)
        out_tok = (si == len(stages) - 1)
        nc = _prog(len(layers), in_tok, out_tok)
        ws = {
            "w_in": np.ascontiguousarray(inp["w_in"][layers], dtype=np.float32),
            "w_o": np.ascontiguousarray(inp["w_o"][layers], dtype=np.float32),
            "w1": np.ascontiguousarray(inp["w_mlp_in"][layers], dtype=np.float32),
            "w2": np.ascontiguousarray(inp["w_mlp_out"][layers], dtype=np.float32),
            "vecs": pack_vecs(inp, layers),
        }
        in_maps = [dict(ws, x=np.ascontiguousarray(cur[b])) for b in range(ncores)]
        res = run_bass_kernel_spmd(nc, in_maps, core_ids=list(range(ncores)))
        cur = [res.results[b]["out"] for b in range(ncores)]
    return np.stack(cur, axis=0).astype(np.float32)
```

```python
import numpy as np
import concourse.bass as bass
import concourse.mybir as mybir
from concourse.bass_utils import run_bass_kernel_spmd

F32 = mybir.dt.float32
BF16 = mybir.dt.bfloat16
AF = mybir.ActivationFunctionType
ALU = mybir.AluOpType

SEQ = 4096
D = 1024
T = 512
NT = SEQ // T
NSLAB = 23
NSLOT = 3
EPS = 1e-6
NVL = 186
NEG = -30000.0

SL_Q, SL_K, SL_V, SL_A, SL_G, SL_O0, SL_W1, SL_W2 = 0, 1, 2, 3, 4, 5, 7, 15


class Op:
    __slots__ = ("eng", "fn", "cdeps", "ddeps", "idx", "sig", "dsem", "dcount", "sigidx", "group")


class Sched:
    ENGS = ("pe", "act", "dve", "pool", "sp")

    def __init__(self):
        self.q = {e: [] for e in self.ENGS}
        self.lw = {}
        self.rd = {}
        self.dcnt = {}

    def add(self, eng, fn, reads=(), writes=(), dsem=None, group=None):
        op = Op()
        op.eng = eng
        op.fn = fn
        op.sig = False
        op.dsem = dsem
        op.dcount = 0
        op.group = group
        deps = []
        for r in reads:
            w = self.lw.get(r)
            if w is not None:
                deps.append(w)
        for r in writes:
            w = self.lw.get(r)
            if w is not None and not (group is not None and w.group == group):
                deps.append(w)
            rr = self.rd.get(r)
            if rr:
                deps.extend(rr.values())
        cd = {}
        dd = {}
        for d_ in deps:
            if d_.dsem is not None:
                if dd.get(d_.dsem, 0) < d_.dcount:
                    dd[d_.dsem] = d_.dcount
            else:
                if d_.eng == "pe" and eng == "pe":
                    continue
                if cd.get(d_.eng, -1) < d_.idx:
                    cd[d_.eng] = d_.idx
                    d_.sig = True
        op.cdeps = cd
        op.ddeps = dd
        op.idx = len(self.q[eng])
        self.q[eng].append(op)
        if dsem is not None:
            self.dcnt[dsem] = self.dcnt.get(dsem, 0) + 16
            op.dcount = self.dcnt[dsem]
        for r in reads:
            if r in writes:
                continue
            m = self.rd.setdefault(r, {})
            key = eng if dsem is None else ("dma", id(op))
            m[key] = op
        for r in writes:
            self.lw[r] = op
            self.rd[r] = {}
        return op

    def emit(self, eng, e, esems, dsems):
        ops = self.q[eng]
        known = {}
        for op in ops:
            for pe_, idx in op.cdeps.items():
                tgt = self.q[pe_][idx]
                val = tgt.sigidx
                if known.get(pe_, 0) < val:
                    e.wait_ge(esems[pe_], val)
                    known[pe_] = val
            for ds, cnt in op.ddeps.items():
                if known.get(ds, 0) < cnt:
                    e.wait_ge(dsems[ds], cnt)
                    known[ds] = cnt
            ins = op.fn(e)
            if op.dsem is not None:
                ins.then_inc(dsems[op.dsem], 16)
            elif op.sig:
                ins.then_inc(esems[eng], 1)

    def finalize(self):
        for eng in self.ENGS:
            n = 0
            for op in self.q[eng]:
                if op.sig and op.dsem is None:
                    n += 1
                    op.sigidx = n
                else:
                    op.sigidx = None


def build_program(nl, in_tok, out_tok):
    nc = bass.Bass("TRN2", target_bir_lowering=False)
    if in_tok:
        x_in = nc.dram_tensor("x", [SEQ, D], F32, kind="ExternalInput").ap()
    else:
        x_in = nc.dram_tensor("x", [NT, 128, 8 * T], F32, kind="ExternalInput").ap()
    w_in = nc.dram_tensor("w_in", [nl, D, 2568], F32, kind="ExternalInput").ap()
    w_o = nc.dram_tensor("w_o", [nl, D, D], F32, kind="ExternalInput").ap()
    w1 = nc.dram_tensor("w1", [nl, D, 4 * D], F32, kind="ExternalInput").ap()
    w2 = nc.dram_tensor("w2", [nl, 4 * D, D], F32, kind="ExternalInput").ap()
    vecs_d = nc.dram_tensor("vecs", [128, nl * NVL], F32, kind="ExternalInput").ap()
    if out_tok:
        y_out = nc.dram_tensor("out", [SEQ, D], F32, kind="ExternalOutput").ap()
    else:
        y_out = nc.dram_tensor("out", [NT, 128, 8 * T], F32, kind="ExternalOutput").ap()
    wsc = nc.dram_tensor("wsc", [nl, NSLAB, 128, 4096], BF16, kind="Internal").ap()
    xmid = nc.dram_tensor("xmid", [max(nl - 1, 1), NT, 128, 8 * T], F32, kind="Internal").ap()

    S = Sched()
    from contextlib import ExitStack
    with ExitStack() as es:
        def sb(name, shape, dt):
            return es.enter_context(nc.sbuf_tensor(name, shape, dt))

        def sem(name):
            return es.enter_context(nc.semaphore(name))

        Kc = sb("Kc", [128, 4 * SEQ], BF16)
        Vc = sb("Vc", [128, 32 * 520], BF16)
        Call = sb("Call", [128, 32 * 8], F32)
        BT = sb("BT", [128, 32 * 8], F32)
        carry = sb("carry", [128, 8], F32)
        cmid = sb("cmid", [128, 8], F32)
        zf = sb("zf", [128, 32], F32)
        azf = sb("azf", [128, 32], F32)
        ef = sb("ef", [128, 32], F32)
        mzf = sb("mzf", [128, 32], F32)
        logf = sb("logf", [128, 32], F32)
        X2 = [sb("xA", [128, 8 * T], F32), sb("xB", [128, 8 * T], F32)]
        stg = [sb(f"stg{k}", [128, D], F32) for k in range(2)]
        uT = sb("uT", [128, 8 * T], BF16)
        slots = [sb(f"slot{k}", [128, 4096], BF16) for k in range(NSLOT)]
        qT = sb("qT", [128, 8 * T], BF16)
        sq2 = [sb(f"sq2_{k}", [128, T], BF16) for k in range(2)]
        hin = sb("hin", [128, 4 * 542], BF16)
        dg = [sb(f"dg{k}", [128, 128], BF16) for k in range(8)]
        TR = sb("TR", [128, 16384], BF16)
        TRf = TR.bitcast(F32)
        rs_t = [sb(f"rs{k}", [128, T], F32) for k in range(2)]
        Pb = [sb(f"P{k}", [128, T], BF16) for k in range(3)]
        Osb = [sb(f"Osb{k}", [128, T], F32) for k in range(2)]
        rinv = sb("rinv", [128, T], F32)
        vecs = sb("vecs_sb", [128, nl * NVL], F32)
        wf = sb("wf", [128, nl * 64], BF16)
        ident_f = sb("ident_f", [128, 128], F32)
        ident_b = sb("ident_b", [128, 128], BF16)
        negmask = sb("negmask", [128, 128], BF16)
        tri_f = sb("tri_f", [128, 128], F32)
        ones_f = sb("ones_f", [128, 128], F32)
        ones_b = sb("ones_b", [128, 128], BF16)
        blk_b = sb("blk_b", [128, 128], BF16)
        mean_f = sb("mean_f", [128, 128], F32)
        mean_b = sb("mean_b", [128, 128], BF16)
        sel_a = sb("sel_a", [128, 128], F32)
        sel_b = sb("sel_b", [128, 128], F32)
        epsc = sb("epsc", [128, 2], F32)
        ps = [es.enter_context(nc.psum_tensor(f"ps{k}", [128, T], F32)) for k in range(8)]

        def hT(fc):
            return TR[:, fc * T:(fc + 1) * T]

        def sig_v(cc):
            return TRf[:, cc * T:(cc + 1) * T]

        def acc_v(cc):
            return TRf[:, 2048 + cc * T: 2048 + (cc + 1) * T]

        def mix_v(m):
            return TR[:, 8192 + m * T: 8192 + (m + 1) * T]

        sqb = TR[:, 12288:16384]
        R_SIG = lambda cc: [("TR", 2 * cc), ("TR", 2 * cc + 1)]
        R_ACC = lambda cc: [("TR", 8 + 2 * cc), ("TR", 8 + 2 * cc + 1)]
        R_MIX = lambda m: [("TR", 16 + m)]
        R_SQB = [("TR", 24 + c) for c in range(8)]
        R_HT = lambda fc: [("TR", fc)]

        esems = {e: sem("e_" + e) for e in Sched.ENGS}
        dsem_names = ["vec", "wfd", "xin0", "xin1", "xout0", "xout1"] + [f"xf{c}" for c in range(8)] + [f"of{c}" for c in range(8)] + \
            [f"slot{k}" for k in range(NSLOT)] + [f"pp{l}_{g}" for l in range(nl) for g in range(10)]
        dsems = {n: sem("d_" + n) for n in dsem_names}

        def A(eng, method, reads, writes, *args, **kw):
            dsem = kw.pop("dsem", None)
            group = kw.pop("group", None)
            return S.add(eng, (lambda e: getattr(e, method)(*args, **kw)), reads, writes, dsem=dsem, group=group)

        def MM(out, lhsT, rhs, start, stop, reads, writes):
            return S.add("pe", (lambda e: e.matmul(out, lhsT, rhs, start=start, stop=stop)), reads, writes)

        ring = [[0, 1, 2, 3, 4, 5]]
        CONV_BANK = 6
        ring_pos = [0]

        def next_bank():
            b = ring[0][ring_pos[0] % len(ring[0])]
            ring_pos[0] += 1
            return b
        STAT = [[6, 7]]
        stat_pos = [0]

        def next_stat():
            b = STAT[0][stat_pos[0] % len(STAT[0])]
            stat_pos[0] += 1
            return b
        rs_pos = [0]
        dg_pos = [0]

        def PSr(b):
            return [("ps", b)]

        A("pool", "memset", [], ["ident_f"], ident_f[:], 0.0)
        A("pool", "affine_select", ["ident_f"], ["ident_f"], out=ident_f[:], in_=ident_f[:], pattern=[[-1, 128]],
          compare_op=ALU.not_equal, fill=1.0, base=0, channel_multiplier=1)
        A("pool", "tensor_copy", ["ident_f"], ["ident_b"], out=ident_b[:], in_=ident_f[:])
        A("pool", "memset", [], ["negmask"], negmask[:], 0.0)
        A("pool", "affine_select", ["negmask"], ["negmask"], out=negmask[:], in_=negmask[:], pattern=[[1, 128]],
          compare_op=ALU.is_ge, fill=NEG, base=0, channel_multiplier=-1)
        A("pool", "memset", [], ["tri_f"], tri_f[:], 1.0)
        A("pool", "affine_select", ["tri_f"], ["tri_f"], out=tri_f[:], in_=tri_f[:], pattern=[[1, 128]],
          compare_op=ALU.is_ge, fill=0.0, base=0, channel_multiplier=-1)
        A("pool", "memset", [], ["ones_f"], ones_f[:], 1.0)
        A("pool", "memset", [], ["ones_b"], ones_b[:], 1.0)
        A("pool", "memset", [], ["blk_b"], blk_b[:], 0.0)
        A("pool", "memset", ["blk_b"], ["blk_b"], blk_b[0:64, 0:64], 1.0)
        A("pool", "memset", ["blk_b"], ["blk_b"], blk_b[64:128, 64:128], 1.0)
        A("pool", "memset", [], ["mean_f"], mean_f[:], 1.0 / 512.0)
        A("pool", "memset", [], ["mean_b"], mean_b[:], 1.0 / 512.0)
        A("pool", "memset", [], ["sel_a"], sel_a[:], 0.0)
        A("pool", "memset", [], ["sel_b"], sel_b[:], 0.0)
        A("pool", "affine_select", ["sel_a"], ["sel_a"], out=sel_a[:, 0:64], in_=sel_a[:, 0:64], pattern=[[0, 64]],
          compare_op=ALU.not_equal, fill=1.0, base=-64, channel_multiplier=1)
        A("pool", "affine_select", ["sel_b"], ["sel_b"], out=sel_b[:, 64:128], in_=sel_b[:, 64:128], pattern=[[0, 64]],
          compare_op=ALU.not_equal, fill=1.0, base=-63, channel_multiplier=1)
        A("pool", "memset", [], ["epsc"], epsc[:, 0:1], EPS)
        A("pool", "memset", ["epsc"], ["epsc"], epsc[:, 1:2], 64.0 * EPS)
        A("pool", "memset", [], [("Osb", 0)], Osb[0][:], 0.0)
        A("pool", "memset", [], [("Osb", 1)], Osb[1][:], 0.0)
        A("pool", "memset", [], [("qT", h_) for h_ in range(8)], qT[:], 0.0)
        A("pool", "memset", [], [("Vones",)], bass.AP(Vc, 64, [[32 * 520, 128], [65, 256], [1, 1]]), 1.0)

        A("sp", "dma_start", [], ["vecs"], out=vecs[:], in_=vecs_d, dsem="vec")
        for lp in range(nl):
            A("pool", "dma_start", [], ["wf"], out=bass.AP(wf, lp * 64, [[nl * 64, 128], [8, 8], [1, 8]]),
              in_=w_in[lp, :, 1536:1544].rearrange("(c p) n -> p c n", p=128), dsem="wfd")

        def slab_parts(lp, s_):
            parts = []
            if s_ < 5:
                c0 = [0, 512, 1024, 1544, 2056][s_]
                parts.append((0, 4096, ("p (c n) -> p c n", dict(c=8)),
                              w_in[lp, :, c0:c0 + 512].rearrange("(c p) n -> p c n", p=128)))
            elif s_ < 7:
                h_ = s_ - SL_O0
                parts.append((0, 4096, ("p (c n) -> p c n", dict(c=8)),
                              w_o[lp, :, h_ * 512:(h_ + 1) * 512].rearrange("(c p) n -> p c n", p=128)))
            elif s_ < 15:
                f_ = s_ - SL_W1
                parts.append((0, 4096, ("p (c n) -> p c n", dict(c=8)),
                              w1[lp, :, f_ * 512:(f_ + 1) * 512].rearrange("(c p) n -> p c n", p=128)))
            else:
                dc = s_ - SL_W2
                for qd in range(4):
                    parts.append((qd * 1024, 1024, ("p (f n) -> p f n", dict(f=8)),
                                  w2[lp, qd * 1024:(qd + 1) * 1024, dc * 128:(dc + 1) * 128].rearrange("(f p) n -> p f n", p=128)))
            return parts

        GROUP_OF = lambda s_: s_ if s_ < 5 else (5 if s_ < 7 else (6 + (s_ - 7) // 4 if s_ < 15 else 8 + (s_ - 15) // 4))

        def prepass(lp):
            for s_ in range(NSLAB):
                for off, ln, (rs_, kw), src in slab_parts(lp, s_):
                    A("pool", "dma_start", [], [("wsc", lp, GROUP_OF(s_))],
                      out=wsc[lp, s_, :, off:off + ln].rearrange(rs_, **kw), in_=src, dsem=f"pp{lp}_{GROUP_OF(s_)}",
                      group=("pp", lp))

        slab_ctr = [0]
        preloaded = {}

        def preload_slab(lp, i, s_):
            k = slab_ctr[0] % NSLOT
            slab_ctr[0] += 1
            if lp == 0 and i == 0:
                for off, ln, (rs_, kw), src in slab_parts(0, s_):
                    A("pool", "dma_start", [], [("slot", k)], out=slots[k][:, off:off + ln].rearrange(rs_, **kw), in_=src,
                      dsem=f"slot{k}", group=("fill", slab_ctr[0]))
                A("sp", "dma_start", [("slot", k)], [("wsc", 0, GROUP_OF(s_))], out=wsc[0, s_], in_=slots[k][:],
                  dsem=f"pp0_{GROUP_OF(s_)}")
            else:
                A("sp", "dma_start", [("wsc", lp, GROUP_OF(s_))], [("slot", k)], out=slots[k][:], in_=wsc[lp, s_],
                  dsem=f"slot{k}")
            preloaded[(lp, i, s_)] = k

        def load_slab(lp, s_):
            key = (lp, cur_tile[0], s_)
            if key not in preloaded:
                preload_slab(lp, cur_tile[0], s_)
            return preloaded.pop(key)

        cur_tile = [0]

        def V(lp, off, n):
            return vecs[:, lp * NVL + off: lp * NVL + off + n]

        def rms_sq_chunk(c, b, xt, XK):
            A("act", "activation", [XK(c)], [("TR", 24 + c)], out=sqb[:, c * T:(c + 1) * T], in_=xt[:, c * T:(c + 1) * T],
              func=AF.Square)
            MM(ps[b][:], ones_b[:], sqb[:, c * T:(c + 1) * T], c == 0, c == 7, [("TR", 24 + c), "ones_b"], PSr(b))

        def rms_finish(lp, goff, b, xt, XK):
            k = rs_pos[0] % 2
            rs_pos[0] += 1
            A("act", "activation", PSr(b) + ["epsc"], [("rs", k)], out=rs_t[k][:], in_=ps[b][:], func=AF.Ln,
              bias=epsc[:, 0:1], scale=1.0 / D)
            A("act", "activation", [("rs", k)], [("rs", k)], out=rs_t[k][:], in_=rs_t[k][:], func=AF.Exp, scale=-0.5)
            for c in range(8):
                A("dve", "scalar_tensor_tensor", [XK(c), ("rs", k), "vecs"], [("uT", c)],
                  out=uT[:, c * T:(c + 1) * T], in0=xt[:, c * T:(c + 1) * T], scalar=V(lp, goff + c, 1),
                  in1=rs_t[k][:], op0=ALU.mult, op1=ALU.mult)

        def prefetch_dma(lp, i, par):
            xt = X2[par]
            XK = lambda c: ("xT", par, c)
            if lp == 0 and in_tok:
                for r in range(2):
                    A("sp", "dma_start", [], [("stg", r)], out=stg[r][:], in_=x_in[i * T + r * 128: i * T + (r + 1) * 128, :],
                      dsem=f"xin{r}")
            elif lp == 0:
                for c in range(8):
                    A("sp", "dma_start", [], [XK(c)], out=xt[:, c * T:(c + 1) * T], in_=x_in[i, :, c * T:(c + 1) * T],
                      dsem=f"xf{c}")
            else:
                for c in range(8):
                    A("sp", "dma_start", [("xmid", lp - 1, i, c)], [XK(c)], out=xt[:, c * T:(c + 1) * T],
                      in_=xmid[lp - 1, i, :, c * T:(c + 1) * T], dsem=f"xf{c}")

        def prefetch_stages(lp, i, par):
            xt = X2[par]
            XK = lambda c: ("xT", par, c)
            stages = []
            if lp == 0 and in_tok:
                def tr_stage(r):
                    k = r % 2
                    for hf in range(2):
                        b = next_bank()
                        for cq in range(4):
                            c = hf * 4 + cq
                            S.add("pe", (lambda e, b=b, cq=cq, c=c, k=k: e.transpose(
                                ps[b][:, cq * 128:(cq + 1) * 128], stg[k][:, c * 128:(c + 1) * 128], ident_f[:])),
                                [("stg", k), "ident_f"], PSr(b))
                        A("dve", "tensor_copy", PSr(b), [XK(hf * 4 + cq) for cq in range(4)],
                          out=bass.AP(xt, hf * 4 * T + r * 128, [[8 * T, 128], [T, 4], [1, 128]]),
                          in_=bass.AP(ps[b], 0, [[T, 128], [128, 4], [1, 128]]))
                    if r + 2 < 4:
                        A("act", "dma_start", [], [("stg", k)], out=stg[k][:],
                          in_=x_in[i * T + (r + 2) * 128: i * T + (r + 3) * 128, :], dsem=f"xin{k}")
                for r in range(4):
                    stages.append(lambda r=r: tr_stage(r))
            sb_ = {}

            def sq_pair(c0):
                for c in (c0, c0 + 1):
                    A("act", "activation", [XK(c)], [("sq2", c % 2)], out=sq2[c % 2][:], in_=xt[:, c * T:(c + 1) * T],
                      func=AF.Square)

            def mm_pair(c0):
                if c0 == 0:
                    sb_["b"] = next_stat()
                b = sb_["b"]
                for c in (c0, c0 + 1):
                    MM(ps[b][:], ones_b[:], sq2[c % 2][:], c == 0, c == 7, [("sq2", c % 2), "ones_b"], PSr(b))

            def st_a():
                sq_pair(0)

            def st_mid(c0):
                mm_pair(c0)
                sq_pair(c0 + 2)

            def st_end():
                mm_pair(6)
                rms_finish(lp, 0, sb_["b"], xt, XK)
            if stages:
                last_tr = stages.pop()
                stages.append(lambda: (last_tr(), st_a()))
            else:
                stages.append(st_a)
            stages.extend([lambda: st_mid(0), lambda: st_mid(2), lambda: st_mid(4), st_end])
            return stages

        U_ALL = [("uT", c) for c in range(8)]

        def proj_chunk(slot, col0):
            b = next_bank()
            for c in range(8):
                MM(ps[b][:], slots[slot][:, c * 512 + col0: c * 512 + col0 + 128], uT[:, c * T:(c + 1) * T],
                   c == 0, c == 7, [("slot", slot), ("uT", c)], PSr(b))
            return b

        pending_store = []
        prefetch_dma(0, 0, 0)
        for st_ in prefetch_stages(0, 0, 0):
            st_()
        for lp in range(nl):
            A("dve", "memset", [], ["carry"], carry[:], 0.0)
            A("dve", "memset", [], [("hin", cc) for cc in range(4)], hin[:], 0.0)
            last_layer = (lp == nl - 1)
            for i in range(NT):
                t0 = i * T
                cur_tile[0] = i
                par = (lp * NT + i) % 2
                xT = X2[par]
                XK = (lambda par: (lambda c: ("xT", par, c)))(par)
                if i + 1 < NT:
                    nxt = (lp, i + 1)
                elif lp + 1 < nl:
                    nxt = (lp + 1, 0)
                else:
                    nxt = None
                if i == 1 and lp + 1 < nl:
                    prepass(lp + 1)

                def qk_post(kind, p, b, k2):
                    sbk = next_stat()
                    MM(ps[sbk][:], blk_b[:], sq2[k2][:], True, True, [("sq2", k2), "blk_b"], PSr(sbk))
                    k = rs_pos[0] % 2
                    rs_pos[0] += 1
                    if kind == 0:
                        A("act", "activation", PSr(sbk) + ["epsc"], [("rs", k)], out=rs_t[k][:], in_=ps[sbk][:],
                          func=AF.Ln, bias=epsc[:, 1:2], scale=1.0)
                    else:
                        A("act", "activation", PSr(sbk) + ["epsc"], [("rs", k)], out=rs_t[k][:], in_=ps[sbk][:],
                          func=AF.Ln, bias=epsc[:, 0:1], scale=1.0 / 64.0)
                    A("act", "activation", [("rs", k)], [("rs", k)], out=rs_t[k][:], in_=rs_t[k][:], func=AF.Exp, scale=-0.5)
                    if kind == 0:
                        for e_ in range(2):
                            pr = slice(e_ * 64, (e_ + 1) * 64)
                            h_ = 2 * p + e_
                            A("dve", "scalar_tensor_tensor", PSr(b) + [("rs", k), "vecs"], [("qT", h_)],
                              out=qT[pr, h_ * T:(h_ + 1) * T], in0=ps[b][pr, :], scalar=vecs[pr, lp * NVL + 16: lp * NVL + 17],
                              in1=rs_t[k][pr, :], op0=ALU.mult, op1=ALU.mult)
                    else:
                        A("dve", "scalar_tensor_tensor", PSr(b) + [("rs", k), "vecs"], [("Kc", p, i)],
                          out=Kc[:, p * SEQ + t0: p * SEQ + t0 + T], in0=ps[b][:], scalar=V(lp, 17, 1),
                          in1=rs_t[k][:], op0=ALU.mult, op1=ALU.mult)

                pend = None
                qk_slots = [load_slab(lp, SL_Q), load_slab(lp, SL_K)]
                for kind in range(2):
                    for p in range(4):
                        b = proj_chunk(qk_slots[kind], p * 128)
                        k2 = (kind * 4 + p) % 2
                        A("act", "activation", PSr(b), [("sq2", k2)], out=sq2[k2][:], in_=ps[b][:], func=AF.Square)
                        if pend is not None:
                            qk_post(*pend)
                        pend = (kind, p, b, k2)
                qk_post(*pend)

                slot = load_slab(lp, SL_V)
                fb = next_stat()
                for r in range(4):
                    b = next_bank()
                    for c in range(8):
                        MM(ps[b][:], uT[:, c * T + r * 128: c * T + (r + 1) * 128], slots[slot][:, c * 512:(c + 1) * 512],
                           c == 0, c == 7, [("slot", slot), ("uT", c)], PSr(b))
                    blk = i * 4 + r
                    A("dve", "tensor_copy", PSr(b), [("Vc", i, r)],
                      out=bass.AP(Vc, blk * 520, [[32 * 520, 128], [65, 8], [1, 64]]),
                      in_=bass.AP(ps[b], 0, [[T, 128], [64, 8], [1, 64]]))
                    for c in range(8):
                        MM(ps[fb][:, r * 8:(r + 1) * 8], uT[:, c * T + r * 128: c * T + (r + 1) * 128],
                           wf[:, lp * 64 + c * 8: lp * 64 + (c + 1) * 8], c == 0, c == 7, [("uT", c), "wf"], PSr(fb))
                A("dve", "tensor_tensor", PSr(fb) + ["vecs"], ["zf"], out=zf[:], in0=ps[fb][:, 0:32], in1=V(lp, 154, 32),
                  op=ALU.add)
                A("dve", "scalar_tensor_tensor", ["zf"], ["azf"], out=azf[:], in0=zf[:], scalar=-1.0, in1=zf[:],
                  op0=ALU.mult, op1=ALU.max)
                A("act", "activation", ["azf"], ["ef"], out=ef[:], in_=azf[:], func=AF.Exp, scale=-1.0)
                A("act", "activation", ["ef"], ["ef"], out=ef[:], in_=ef[:], func=AF.Ln, bias=1.0, scale=1.0)
                A("dve", "tensor_scalar", ["zf"], ["mzf"], out=mzf[:], in0=zf[:], scalar1=0.0, scalar2=None, op0=ALU.min)
                A("dve", "tensor_tensor", ["mzf", "ef"], ["logf"], out=logf[:], in0=mzf[:], in1=ef[:], op=ALU.subtract)
                slot_a = load_slab(lp, SL_A)
                slot_g = load_slab(lp, SL_G)
                for cc in range(4):
                    ba = proj_chunk(slot_a, cc * 128)
                    bg = proj_chunk(slot_g, cc * 128)
                    A("act", "activation", PSr(bg), R_SIG(cc), out=sig_v(cc), in_=ps[bg][:], func=AF.Sigmoid)
                    A("dve", "tensor_tensor", PSr(ba) + R_SIG(cc), [("hin", cc)], out=hin[:, cc * 542 + 30: cc * 542 + 542],
                      in0=ps[ba][:], in1=sig_v(cc), op=ALU.mult)

                cb = next_stat()
                for r in range(4):
                    MM(ps[cb][:, r * 8:(r + 1) * 8], tri_f[:], logf[:, r * 8:(r + 1) * 8], True, r == 0,
                       ["logf", "tri_f"], PSr(cb))
                    for r2 in range(r):
                        MM(ps[cb][:, r * 8:(r + 1) * 8], ones_f[:], logf[:, r2 * 8:(r2 + 1) * 8], False, r2 == r - 1,
                           ["logf", "ones_f"], PSr(cb))
                for r in range(4):
                    MM(ps[cb][:, 32:40], ones_f[:], logf[:, r * 8:(r + 1) * 8], r == 0, r == 3, ["logf", "ones_f"], PSr(cb))
                for r in range(2):
                    MM(ps[cb][:, 40:48], ones_f[:], logf[:, r * 8:(r + 1) * 8], r == 0, r == 1, ["logf", "ones_f"], PSr(cb))
                A("dve", "tensor_tensor", PSr(cb) + ["carry"], [("Call", i)],
                  out=bass.AP(Call, i * 32, [[256, 128], [8, 4], [1, 8]]),
                  in0=bass.AP(ps[cb], 0, [[T, 128], [8, 4], [1, 8]]),
                  in1=bass.AP(carry, 0, [[8, 128], [0, 4], [1, 8]]), op=ALU.add)
                A("dve", "tensor_tensor", PSr(cb) + ["carry"], ["cmid"], out=cmid[:], in0=ps[cb][:, 40:48], in1=carry[:],
                  op=ALU.add)
                A("dve", "tensor_tensor", PSr(cb) + ["carry"], ["carry"], out=carry[:], in0=ps[cb][:, 32:40], in1=carry[:],
                  op=ALU.add)
                nj = 4 * i + 4
                A("dve", "tensor_tensor", ["cmid"] + [("Call", t) for t in range(i + 1)], ["BT"],
                  out=bass.AP(BT, 0, [[256, 128], [8, nj], [1, 8]]),
                  in0=bass.AP(cmid, 0, [[8, 128], [0, nj], [1, 8]]),
                  in1=bass.AP(Call, 0, [[256, 128], [8, nj], [1, 8]]), op=ALU.subtract)

                conv_thunks = []
                taps = [(cc, w_) for cc in range(4) for w_ in range(31)]
                tap_dg = {}
                built = [0]

                def build_upto(m):
                    while built[0] < min(m, len(taps)):
                        cc, w_ = taps[built[0]]
                        kd = dg_pos[0] % 8
                        dg_pos[0] += 1
                        tap_dg[built[0]] = kd
                        A("dve", "tensor_scalar", ["ident_b", "vecs"], [("dg", kd)], out=dg[kd][:], in0=ident_b[:],
                          scalar1=V(lp, 18 + cc * 31 + w_, 1), scalar2=None, op0=ALU.mult)
                        built[0] += 1

                def mk_conv(cc):
                    base = cc * 542

                    def tap(w_):
                        idx = cc * 31 + w_
                        build_upto(idx + 7)
                        kd = tap_dg[idx]
                        MM(ps[CONV_BANK][:], dg[kd][:], hin[:, base + w_: base + w_ + T], w_ == 0, w_ == 30,
                           [("dg", kd), ("hin", cc)], PSr(CONV_BANK))
                    for w_ in range(31):
                        conv_thunks.append(lambda w_=w_: tap(w_))

                    def fin():
                        A("dve", "tensor_scalar", PSr(CONV_BANK) + ["vecs"], R_ACC(cc), out=acc_v(cc), in0=ps[CONV_BANK][:],
                          scalar1=V(lp, 142 + cc, 1), scalar2=None, op0=ALU.add)
                        A("dve", "tensor_copy", [("hin", cc)], [("hin", cc)], out=hin[:, base: base + 30],
                          in_=hin[:, base + T: base + T + 30])
                        return 9
                    conv_thunks.append(fin)
                for cc in range(4):
                    mk_conv(cc)
                build_upto(7)

                sqd = TR[:, 0:4 * T]
                R_SQD = [("TR", k_) for k_ in range(4)]
                ln_banks = {}

                def ln_mean():
                    mb = CONV_BANK
                    ln_banks["m"] = mb
                    for cc in range(4):
                        MM(ps[mb][:], mean_f[:], acc_v(cc), cc == 0, cc == 3, R_ACC(cc) + ["mean_f"], PSr(mb))
                    return 4

                def ln_center():
                    mb = ln_banks["m"]
                    for cc in range(4):
                        A("dve", "tensor_tensor", PSr(mb) + R_ACC(cc), R_ACC(cc), out=acc_v(cc), in0=acc_v(cc), in1=ps[mb][:],
                          op=ALU.subtract)
                        A("dve", "tensor_tensor", R_ACC(cc), [("TR", cc)], out=sqd[:, cc * T:(cc + 1) * T], in0=acc_v(cc),
                          in1=acc_v(cc), op=ALU.mult)
                    return 18

                def ln_var():
                    vb = CONV_BANK
                    ln_banks["v"] = vb
                    for cc in range(4):
                        MM(ps[vb][:], mean_b[:], sqd[:, cc * T:(cc + 1) * T], cc == 0, cc == 3, [("TR", cc), "mean_b"], PSr(vb))
                    k = rs_pos[0] % 2
                    rs_pos[0] += 1
                    ln_banks["k"] = k
                    A("act", "activation", PSr(vb) + ["epsc"], [("rs", k)], out=rs_t[k][:], in_=ps[vb][:], func=AF.Ln,
                      bias=epsc[:, 0:1], scale=1.0)
                    A("act", "activation", [("rs", k)], [("rs", k)], out=rs_t[k][:], in_=rs_t[k][:], func=AF.Exp, scale=-0.5)
                    for cc in range(4):
                        A("dve", "tensor_tensor", R_ACC(cc) + [("rs", k)], R_ACC(cc), out=acc_v(cc), in0=acc_v(cc),
                          in1=rs_t[k][:], op=ALU.mult)
                    return None

                def ln_finish():
                    for cc in range(4):
                        A("act", "activation", R_ACC(cc) + ["vecs"], R_MIX(4 + cc), out=mix_v(4 + cc), in_=acc_v(cc),
                          func=AF.Silu, bias=V(lp, 150 + cc, 1), scale=V(lp, 146 + cc, 1))
                conv_thunks.extend([ln_mean, ln_center, ln_var])

                steps = []
                for p in range(4):
                    for e_ in range(2):
                        for j in range(nj):
                            c0 = 0 if j < 4 * i else (j - 4 * i) * 128
                            steps.append((p, e_, j, c0))
                LAG = 2
                ring[0] = [0, 1, 2]
                STAT[0] = [7]
                sbank = {}
                nsteps = len(steps)

                def emit_S(n):
                    p, e_, j, c0 = steps[n]
                    h = 2 * p + e_
                    b = next_bank()
                    sbank[n] = b
                    diag = j >= 4 * i
                    MM(ps[b][:, c0:T], Kc[:, p * SEQ + j * 128: p * SEQ + (j + 1) * 128], qT[:, h * T + c0: (h + 1) * T],
                       True, not diag, [("Kc", p, j // 4), ("qT", h)], PSr(b))
                    if diag:
                        MM(ps[b][:, c0:c0 + 128], ident_b[:], negmask[:], False, True, ["ident_b", "negmask"], PSr(b))
                    A("act", "activation", PSr(b) + ["BT"], [("P", n % 3)], out=Pb[n % 3][:, c0:T], in_=ps[b][:, c0:T],
                      func=AF.Exp, bias=BT[:, j * 8 + h: j * 8 + h + 1], scale=1.0)

                def emit_PV(n):
                    p, e_, j, c0 = steps[n]
                    h = 2 * p + e_
                    ob = 3 + h % 3
                    if e_ == 0:
                        lhsT = Vc[:, j * 520 + h * 65: j * 520 + h * 65 + 128]
                        out = ps[ob][:, c0:T]
                    else:
                        lhsT = Vc[:, j * 520 + (h - 1) * 65 + 1: j * 520 + (h - 1) * 65 + 129]
                        out = ps[ob][:, c0:T]
                    MM(out, lhsT, Pb[n % 3][:, c0:T], j == 0, j == nj - 1, [("P", n % 3), ("Vc", j // 4, j % 4), ("Vones",)],
                       PSr(ob))
                    if e_ == 1 and j == nj - 1:
                        emit_norm(p, n)

                deferred = []
                for q_, st_ in enumerate(pending_store):
                    deferred.append((3 + 3 * q_, st_))
                del pending_store[:]

                def emit_norm(p, n):
                    oa, ob_ = 3 + (2 * p) % 3, 3 + (2 * p + 1) % 3
                    A("dve", "tensor_copy", PSr(oa), [("Osb", 0)], out=Osb[0][0:65, :], in_=ps[oa][0:65, :])
                    A("dve", "tensor_copy", PSr(ob_), [("Osb", 1)], out=Osb[1][:], in_=ps[ob_][:])

                    def part2():
                        sbk = next_stat()
                        MM(ps[sbk][:], sel_a[:], Osb[0][:], True, False, [("Osb", 0), "sel_a"], PSr(sbk))
                        MM(ps[sbk][:], sel_b[:], Osb[1][:], False, True, [("Osb", 1), "sel_b"], PSr(sbk))
                        A("dve", "reciprocal", PSr(sbk), ["rinv"], out=rinv[:], in_=ps[sbk][:])
                        A("dve", "tensor_tensor", [("Osb", 0), "rinv"], R_MIX(p), out=mix_v(p)[0:64, :], in0=Osb[0][0:64, :],
                          in1=rinv[0:64, :], op=ALU.mult)
                        A("dve", "tensor_tensor", [("Osb", 1), "rinv"], R_MIX(p), out=mix_v(p)[64:128, :],
                          in0=Osb[1][64:128, :], in1=rinv[64:128, :], op=ALU.mult)
                    deferred.append((n + min(10, 2 * nj - 2), part2))

                per = -(-len(conv_thunks) // max(nsteps - 56, 8))
                hold = [0]
                for n in range(nsteps + LAG):
                    if n < nsteps:
                        emit_S(n)
                    if n - LAG >= 0:
                        emit_PV(n - LAG)
                    for d_ in [d_ for d_ in deferred if d_[0] <= n]:
                        deferred.remove(d_)
                        d_[1]()
                    for _ in range(per):
                        if conv_thunks and n >= hold[0]:
                            th = conv_thunks.pop(0)
                            r_ = th()
                            if r_:
                                hold[0] = n + r_
                                break
                while deferred:
                    deferred.pop(0)[1]()
                while conv_thunks:
                    conv_thunks.pop(0)()
                ln_finish()
                ring[0] = [0, 1, 2, 3, 4, 5]
                STAT[0] = [6, 7]

                n2b = next_stat()
                wo_slots = [load_slab(lp, SL_O0), load_slab(lp, SL_O0 + 1)]
                M_ORDER = [0, 1, 2, 4, 5, 6, 7, 3]

                def wo_mm(dc, b, m, first, last):
                    h_, dq = dc // 4, dc % 4
                    MM(ps[b][:], slots[wo_slots[h_]][:, m * 512 + dq * 128: m * 512 + (dq + 1) * 128], mix_v(m),
                       first, last, [("slot", wo_slots[h_])] + R_MIX(m), PSr(b))

                def wo_add(dc, b):
                    A("dve", "tensor_tensor", PSr(b) + [XK(dc)], [XK(dc)], out=xT[:, dc * T:(dc + 1) * T],
                      in0=xT[:, dc * T:(dc + 1) * T], in1=ps[b][:], op=ALU.add)

                def wo_post(dc):
                    A("dve", "tensor_scalar", [XK(dc), "vecs"], [("uT", dc)], out=uT[:, dc * T:(dc + 1) * T],
                      in0=xT[:, dc * T:(dc + 1) * T], scalar1=V(lp, 8 + dc, 1), scalar2=None, op0=ALU.mult)
                    A("act", "activation", [XK(dc)], [("TR", 24 + dc)], out=sqb[:, dc * T:(dc + 1) * T],
                      in_=xT[:, dc * T:(dc + 1) * T], func=AF.Square)

                def wo_fin(dc, b):
                    wo_add(dc, b)
                    wo_post(dc)

                wo_banks = {}
                for dc in range(6):
                    wo_banks[dc] = next_bank()
                    for mi, m in enumerate(M_ORDER[:7]):
                        wo_mm(dc, wo_banks[dc], m, mi == 0, False)
                for dc in range(6):
                    wo_mm(dc, wo_banks[dc], 3, False, True)
                for dc in range(3):
                    wo_add(dc, wo_banks[dc])
                for dc in range(3):
                    wo_post(dc)
                for dc in range(3, 6):
                    wo_add(dc, wo_banks[dc])
                for dc in range(3, 6):
                    wo_post(dc)
                for dc in range(6, 8):
                    b = next_bank()
                    for mi, m in enumerate(M_ORDER):
                        wo_mm(dc, b, m, mi == 0, mi == 7)
                    wo_fin(dc, b)

                k2_ = rs_pos[0] % 2
                rs_pos[0] += 1

                def norm2_stats():
                    for c in range(8):
                        MM(ps[n2b][:], ones_b[:], sqb[:, c * T:(c + 1) * T], c == 0, c == 7, [("TR", 24 + c), "ones_b"],
                           PSr(n2b))
                    A("act", "activation", PSr(n2b) + ["epsc"], [("rs", k2_)], out=rs_t[k2_][:], in_=ps[n2b][:], func=AF.Ln,
                      bias=epsc[:, 0:1], scale=1.0 / D)
                    A("act", "activation", [("rs", k2_)], [("rs", k2_)], out=rs_t[k2_][:], in_=rs_t[k2_][:], func=AF.Exp,
                      scale=-1.0)
                if nxt is not None:
                    prefetch_dma(nxt[0], nxt[1], 1 - par)
                for s_ in range(8):
                    slot = load_slab(lp, SL_W1 + s_)
                    if s_ == 0:
                        w1b = [next_bank() for _ in range(4)]
                        for fq in range(4):
                            for c in range(7):
                                MM(ps[w1b[fq]][:], slots[slot][:, c * 512 + fq * 128: c * 512 + (fq + 1) * 128],
                                   uT[:, c * T:(c + 1) * T], c == 0, False, [("slot", slot), ("uT", c)], PSr(w1b[fq]))
                        for fq in range(4):
                            MM(ps[w1b[fq]][:], slots[slot][:, 7 * 512 + fq * 128: 7 * 512 + (fq + 1) * 128],
                               uT[:, 7 * T:8 * T], False, True, [("slot", slot), ("uT", 7)], PSr(w1b[fq]))
                    for fq in range(4):
                        fc = s_ * 4 + fq
                        b = w1b[fq] if s_ == 0 else proj_chunk(slot, fq * 128)
                        A("act", "activation", PSr(b), R_HT(fc), out=hT(fc), in_=ps[b][:], func=AF.Relu)
                        A("dve", "tensor_tensor", R_HT(fc), R_HT(fc), out=hT(fc), in0=hT(fc), in1=hT(fc), op=ALU.mult)
                    if s_ == 0:
                        norm2_stats()
                pf = prefetch_stages(nxt[0], nxt[1], 1 - par) if nxt is not None else []
                for dc in range(8):
                    if pf:
                        pf.pop(0)()
                    slot = load_slab(lp, SL_W2 + dc)
                    b = next_bank()
                    for fc in range(32):
                        MM(ps[b][:], slots[slot][:, fc * 128:(fc + 1) * 128], hT(fc), fc == 0, fc == 31,
                           [("slot", slot)] + R_HT(fc), PSr(b))
                    A("dve", "tensor_tensor", PSr(b) + [("rs", k2_)], ["rinv"], out=rinv[:], in0=ps[b][:], in1=rs_t[k2_][:],
                      op=ALU.mult)
                    A("dve", "tensor_tensor", ["rinv", XK(dc)], [XK(dc)], out=xT[:, dc * T:(dc + 1) * T],
                      in0=xT[:, dc * T:(dc + 1) * T], in1=rinv[:], op=ALU.add)
                    if not last_layer:
                        A("act", "dma_start", [XK(dc)], [("xmid", lp, i, dc)], out=xmid[lp, i, :, dc * T:(dc + 1) * T],
                          in_=xT[:, dc * T:(dc + 1) * T], dsem=f"of{dc}")
                    elif not out_tok:
                        A("act", "dma_start", [XK(dc)], [("out", i, dc)], out=y_out[i, :, dc * T:(dc + 1) * T],
                          in_=xT[:, dc * T:(dc + 1) * T], dsem=f"of{dc}")

                if nxt is not None:
                    preload_slab(nxt[0], nxt[1], SL_Q)
                    preload_slab(nxt[0], nxt[1], SL_K)
                if last_layer and out_tok:
                    def store_half(r, hf, i=i, t0=t0, xT=xT, XK=XK):
                        k = r % 2
                        b = next_stat()
                        for cq in range(4):
                            c = hf * 4 + cq
                            S.add("pe", (lambda e, b=b, cq=cq, c=c, r=r, xT=xT: e.transpose(
                                ps[b][:, cq * 128:(cq + 1) * 128], xT[:, c * T + r * 128: c * T + (r + 1) * 128],
                                ident_f[:])), [XK(c), "ident_f"], PSr(b))
                        A("dve", "tensor_copy", PSr(b), [("stg", k)], out=stg[k][:, hf * 512:(hf + 1) * 512], in_=ps[b][:])
                        if hf == 1:
                            A("pool", "dma_start", [("stg", k)], [("out", i, r)],
                              out=y_out[t0 + r * 128: t0 + (r + 1) * 128, :], in_=stg[k][:], dsem=f"xout{k}")
                    for r in range(4):
                        for hf in range(2):
                            pending_store.append(lambda r=r, hf=hf, f=store_half: f(r, hf))

        while pending_store:
            pending_store.pop(0)()

        outs = [r for r in S.lw if isinstance(r, tuple) and r[0] == "out"]
        A("sp", "nop", outs, [])

        S.finalize()
        with nc.Block() as block:
            @block.tensor
            def _(e):
                S.emit("pe", e, esems, dsems)

            @block.scalar
            def _(e):
                S.emit("act", e, esems, dsems)

            @block.vector
            def _(e):
                S.emit("dve", e, esems, dsems)

            @block.gpsimd
            def _(e):
                S.emit("pool", e, esems, dsems)

            @block.sync
            def _(e):
                S.emit("sp", e, esems, dsems)
    return nc


def pack_vecs(inp, layers):
    cols = []
    for l in layers:
        g1 = inp["norm1_g"][l].reshape(8, 128).T
        g2 = inp["norm2_g"][l].reshape(8, 128).T
        gq = np.tile(inp["q_norm_g"][l], 2)[:, None]
        gk = np.tile(inp["k_norm_g"][l], 2)[:, None]
        cw = inp["conv_w"][l].reshape(31, 4, 128).transpose(2, 1, 0).reshape(128, 124)
        cb = inp["conv_b"][l].reshape(4, 128).T
        lg = inp["conv_ln_g"][l].reshape(4, 128).T
        lb = inp["conv_ln_b"][l].reshape(4, 128).T
        bf = np.broadcast_to(np.tile(inp["b_f"][l], 4)[None, :], (128, 32))
        cols += [g1, g2, gq, gk, cw, cb, lg, lb, bf]
    return np.ascontiguousarray(np.concatenate(cols, axis=1).astype(np.float32))


_PROG_CACHE = {}


def _prog(nl, in_tok, out_tok):
    key = (nl, in_tok, out_tok)
    if key not in _PROG_CACHE:
        _PROG_CACHE[key] = build_program(nl, in_tok, out_tok)
    return _PROG_CACHE[key]


FUSED = True


def kernel(**inputs):
    inp = {k: np.asarray(v) for k, v in inputs.items()}
    x = np.ascontiguousarray(inp["x"], dtype=np.float32)
    ncores = 8
    depth = inp["w_in"].shape[0]
    if FUSED:
        stages = [list(range(depth))]
    else:
        stages = [[l] for l in range(depth)]
    cur = [x[b] for b in range(ncores)]
    for si, layers in enumerate(stages):
        in_tok = (si == 0)
        out_tok = (si == len(stages) - 1)
        nc = _prog(len(layers), in_tok, out_tok)
        ws = {
            "w_in": np.ascontiguousarray(inp["w_in"][layers], dtype=np.float32),
            "w_o": np.ascontiguousarray(inp["w_o"][layers], dtype=np.float32),
            "w1": np.ascontiguousarray(inp["w_mlp_in"][layers], dtype=np.float32),
            "w2": np.ascontiguousarray(inp["w_mlp_out"][layers], dtype=np.float32),
            "vecs": pack_vecs(inp, layers),
        }
        in_maps = [dict(ws, x=np.ascontiguousarray(cur[b])) for b in range(ncores)]
        res = run_bass_kernel_spmd(nc, in_maps, core_ids=list(range(ncores)))
        cur = [res.results[b]["out"] for b in range(ncores)]
    return np.stack(cur, axis=0).astype(np.float32)
```

```python
import numpy as np
import concourse.bass as bass
import concourse.mybir as mybir
from concourse.bass_utils import run_bass_kernel_spmd

F32 = mybir.dt.float32
BF16 = mybir.dt.bfloat16
AF = mybir.ActivationFunctionType
ALU = mybir.AluOpType

SEQ = 4096
D = 1024
T = 512
NT = SEQ // T
NSLAB = 23
NSLOT = 3
EPS = 1e-6
NVL = 186
NEG = -30000.0

SL_Q, SL_K, SL_V, SL_A, SL_G, SL_O0, SL_W1, SL_W2 = 0, 1, 2, 3, 4, 5, 7, 15


class Op:
    __slots__ = ("eng", "fn", "cdeps", "ddeps", "idx", "sig", "dsem", "dcount", "sigidx", "group")


class Sched:
    ENGS = ("pe", "act", "dve", "pool", "sp")

    def __init__(self):
        self.q = {e: [] for e in self.ENGS}
        self.lw = {}
        self.rd = {}
        self.dcnt = {}

    def add(self, eng, fn, reads=(), writes=(), dsem=None, group=None):
        op = Op()
        op.eng = eng
        op.fn = fn
        op.sig = False
        op.dsem = dsem
        op.dcount = 0
        op.group = group
        deps = []
        for r in reads:
            w = self.lw.get(r)
            if w is not None:
                deps.append(w)
        for r in writes:
            w = self.lw.get(r)
            if w is not None and not (group is not None and w.group == group):
                deps.append(w)
            rr = self.rd.get(r)
            if rr:
                deps.extend(rr.values())
        cd = {}
        dd = {}
        for d_ in deps:
            if d_.dsem is not None:
                if dd.get(d_.dsem, 0) < d_.dcount:
                    dd[d_.dsem] = d_.dcount
            else:
                if d_.eng == "pe" and eng == "pe":
                    continue
                if cd.get(d_.eng, -1) < d_.idx:
                    cd[d_.eng] = d_.idx
                    d_.sig = True
        op.cdeps = cd
        op.ddeps = dd
        op.idx = len(self.q[eng])
        self.q[eng].append(op)
        if dsem is not None:
            self.dcnt[dsem] = self.dcnt.get(dsem, 0) + 16
            op.dcount = self.dcnt[dsem]
        for r in reads:
            if r in writes:
                continue
            m = self.rd.setdefault(r, {})
            key = eng if dsem is None else ("dma", id(op))
            m[key] = op
        for r in writes:
            self.lw[r] = op
            self.rd[r] = {}
        return op

    def emit(self, eng, e, esems, dsems):
        ops = self.q[eng]
        known = {}
        for op in ops:
            for pe_, idx in op.cdeps.items():
                tgt = self.q[pe_][idx]
                val = tgt.sigidx
                if known.get(pe_, 0) < val:
                    e.wait_ge(esems[pe_], val)
                    known[pe_] = val
            for ds, cnt in op.ddeps.items():
                if known.get(ds, 0) < cnt:
                    e.wait_ge(dsems[ds], cnt)
                    known[ds] = cnt
            ins = op.fn(e)
            if op.dsem is not None:
                ins.then_inc(dsems[op.dsem], 16)
            elif op.sig:
                ins.then_inc(esems[eng], 1)

    def finalize(self):
        for eng in self.ENGS:
            n = 0
            for op in self.q[eng]:
                if op.sig and op.dsem is None:
                    n += 1
                    op.sigidx = n
                else:
                    op.sigidx = None


def build_program(nl, in_tok, out_tok):
    nc = bass.Bass("TRN2", target_bir_lowering=False)
    if in_tok:
        x_in = nc.dram_tensor("x", [SEQ, D], F32, kind="ExternalInput").ap()
    else:
        x_in = nc.dram_tensor("x", [NT, 128, 8 * T], F32, kind="ExternalInput").ap()
    w_in = nc.dram_tensor("w_in", [nl, D, 2568], F32, kind="ExternalInput").ap()
    w_o = nc.dram_tensor("w_o", [nl, D, D], F32, kind="ExternalInput").ap()
    w1 = nc.dram_tensor("w1", [nl, D, 4 * D], F32, kind="ExternalInput").ap()
    w2 = nc.dram_tensor("w2", [nl, 4 * D, D], F32, kind="ExternalInput").ap()
    vecs_d = nc.dram_tensor("vecs", [128, nl * NVL], F32, kind="ExternalInput").ap()
    if out_tok:
        y_out = nc.dram_tensor("out", [SEQ, D], F32, kind="ExternalOutput").ap()
    else:
        y_out = nc.dram_tensor("out", [NT, 128, 8 * T], F32, kind="ExternalOutput").ap()
    wsc = nc.dram_tensor("wsc", [nl, NSLAB, 128, 4096], BF16, kind="Internal").ap()
    xmid = nc.dram_tensor("xmid", [max(nl - 1, 1), NT, 128, 8 * T], F32, kind="Internal").ap()

    S = Sched()
    from contextlib import ExitStack
    with ExitStack() as es:
        def sb(name, shape, dt):
            return es.enter_context(nc.sbuf_tensor(name, shape, dt))

        def sem(name):
            return es.enter_context(nc.semaphore(name))

        Kc = sb("Kc", [128, 4 * SEQ], BF16)
        Vc = sb("Vc", [128, 32 * 520], BF16)
        Call = sb("Call", [128, 32 * 8], F32)
        BT = sb("BT", [128, 32 * 8], F32)
        carry = sb("carry", [128, 8], F32)
        cmid = sb("cmid", [128, 8], F32)
        zf = sb("zf", [128, 32], F32)
        azf = sb("azf", [128, 32], F32)
        ef = sb("ef", [128, 32], F32)
        mzf = sb("mzf", [128, 32], F32)
        logf = sb("logf", [128, 32], F32)
        X2 = [sb("xA", [128, 8 * T], F32), sb("xB", [128, 8 * T], F32)]
        stg = [sb(f"stg{k}", [128, D], F32) for k in range(2)]
        uT = sb("uT", [128, 8 * T], BF16)
        slots = [sb(f"slot{k}", [128, 4096], BF16) for k in range(NSLOT)]
        qT = sb("qT", [128, 8 * T], BF16)
        sq2 = [sb(f"sq2_{k}", [128, T], BF16) for k in range(2)]
        hin = sb("hin", [128, 4 * 542], BF16)
        dg = [sb(f"dg{k}", [128, 128], BF16) for k in range(8)]
        TR = sb("TR", [128, 16384], BF16)
        TRf = TR.bitcast(F32)
        rs_t = [sb(f"rs{k}", [128, T], F32) for k in range(2)]
        Pb = [sb(f"P{k}", [128, T], BF16) for k in range(3)]
        Osb = [sb(f"Osb{k}", [128, T], F32) for k in range(2)]
        rinv = sb("rinv", [128, T], F32)
        vecs = sb("vecs_sb", [128, nl * NVL], F32)
        wf = sb("wf", [128, nl * 64], BF16)
        ident_f = sb("ident_f", [128, 128], F32)
        ident_b = sb("ident_b", [128, 128], BF16)
        negmask = sb("negmask", [128, 128], BF16)
        tri_f = sb("tri_f", [128, 128], F32)
        ones_f = sb("ones_f", [128, 128], F32)
        ones_b = sb("ones_b", [128, 128], BF16)
        blk_b = sb("blk_b", [128, 128], BF16)
        mean_f = sb("mean_f", [128, 128], F32)
        mean_b = sb("mean_b", [128, 128], BF16)
        sel_a = sb("sel_a", [128, 128], F32)
        sel_b = sb("sel_b", [128, 128], F32)
        epsc = sb("epsc", [128, 2], F32)
        ps = [es.enter_context(nc.psum_tensor(f"ps{k}", [128, T], F32)) for k in range(8)]

        def hT(fc):
            return TR[:, fc * T:(fc + 1) * T]

        def sig_v(cc):
            return TRf[:, cc * T:(cc + 1) * T]

        def acc_v(cc):
            return TRf[:, 2048 + cc * T: 2048 + (cc + 1) * T]

        def mix_v(m):
            return TR[:, 8192 + m * T: 8192 + (m + 1) * T]

        sqb = TR[:, 12288:16384]
        R_SIG = lambda cc: [("TR", 2 * cc), ("TR", 2 * cc + 1)]
        R_ACC = lambda cc: [("TR", 8 + 2 * cc), ("TR", 8 + 2 * cc + 1)]
        R_MIX = lambda m: [("TR", 16 + m)]
        R_SQB = [("TR", 24 + c) for c in range(8)]
        R_HT = lambda fc: [("TR", fc)]

        esems = {e: sem("e_" + e) for e in Sched.ENGS}
        dsem_names = ["vec", "wfd", "xin0", "xin1", "xout0", "xout1"] + [f"xf{c}" for c in range(8)] + [f"of{c}" for c in range(8)] + \
            [f"slot{k}" for k in range(NSLOT)] + [f"pp{l}_{g}" for l in range(nl) for g in range(10)]
        dsems = {n: sem("d_" + n) for n in dsem_names}

        def A(eng, method, reads, writes, *args, **kw):
            dsem = kw.pop("dsem", None)
            group = kw.pop("group", None)
            return S.add(eng, (lambda e: getattr(e, method)(*args, **kw)), reads, writes, dsem=dsem, group=group)

        def MM(out, lhsT, rhs, start, stop, reads, writes):
            return S.add("pe", (lambda e: e.matmul(out, lhsT, rhs, start=start, stop=stop)), reads, writes)

        ring = [[0, 1, 2, 3, 4, 5]]
        CONV_BANK = 6
        ring_pos = [0]

        def next_bank():
            b = ring[0][ring_pos[0] % len(ring[0])]
            ring_pos[0] += 1
            return b
        STAT = [[6, 7]]
        stat_pos = [0]

        def next_stat():
            b = STAT[0][stat_pos[0] % len(STAT[0])]
            stat_pos[0] += 1
            return b
        rs_pos = [0]
        dg_pos = [0]

        def PSr(b):
            return [("ps", b)]

        A("pool", "memset", [], ["ident_f"], ident_f[:], 0.0)
        A("pool", "affine_select", ["ident_f"], ["ident_f"], out=ident_f[:], in_=ident_f[:], pattern=[[-1, 128]],
          compare_op=ALU.not_equal, fill=1.0, base=0, channel_multiplier=1)
        A("pool", "tensor_copy", ["ident_f"], ["ident_b"], out=ident_b[:], in_=ident_f[:])
        A("pool", "memset", [], ["negmask"], negmask[:], 0.0)
        A("pool", "affine_select", ["negmask"], ["negmask"], out=negmask[:], in_=negmask[:], pattern=[[1, 128]],
          compare_op=ALU.is_ge, fill=NEG, base=0, channel_multiplier=-1)
        A("pool", "memset", [], ["tri_f"], tri_f[:], 1.0)
        A("pool", "affine_select", ["tri_f"], ["tri_f"], out=tri_f[:], in_=tri_f[:], pattern=[[1, 128]],
          compare_op=ALU.is_ge, fill=0.0, base=0, channel_multiplier=-1)
        A("pool", "memset", [], ["ones_f"], ones_f[:], 1.0)
        A("pool", "memset", [], ["ones_b"], ones_b[:], 1.0)
        A("pool", "memset", [], ["blk_b"], blk_b[:], 0.0)
        A("pool", "memset", ["blk_b"], ["blk_b"], blk_b[0:64, 0:64], 1.0)
        A("pool", "memset", ["blk_b"], ["blk_b"], blk_b[64:128, 64:128], 1.0)
        A("pool", "memset", [], ["mean_f"], mean_f[:], 1.0 / 512.0)
        A("pool", "memset", [], ["mean_b"], mean_b[:], 1.0 / 512.0)
        A("pool", "memset", [], ["sel_a"], sel_a[:], 0.0)
        A("pool", "memset", [], ["sel_b"], sel_b[:], 0.0)
        A("pool", "affine_select", ["sel_a"], ["sel_a"], out=sel_a[:, 0:64], in_=sel_a[:, 0:64], pattern=[[0, 64]],
          compare_op=ALU.not_equal, fill=1.0, base=-64, channel_multiplier=1)
        A("pool", "affine_select", ["sel_b"], ["sel_b"], out=sel_b[:, 64:128], in_=sel_b[:, 64:128], pattern=[[0, 64]],
          compare_op=ALU.not_equal, fill=1.0, base=-63, channel_multiplier=1)
        A("pool", "memset", [], ["epsc"], epsc[:, 0:1], EPS)
        A("pool", "memset", ["epsc"], ["epsc"], epsc[:, 1:2], 64.0 * EPS)
        A("pool", "memset", [], [("Osb", 0)], Osb[0][:], 0.0)
        A("pool", "memset", [], [("Osb", 1)], Osb[1][:], 0.0)
        A("pool", "memset", [], [("qT", h_) for h_ in range(8)], qT[:], 0.0)
        A("pool", "memset", [], [("Vones",)], bass.AP(Vc, 64, [[32 * 520, 128], [65, 256], [1, 1]]), 1.0)

        A("sp", "dma_start", [], ["vecs"], out=vecs[:], in_=vecs_d, dsem="vec")
        for lp in range(nl):
            A("pool", "dma_start", [], ["wf"], out=bass.AP(wf, lp * 64, [[nl * 64, 128], [8, 8], [1, 8]]),
              in_=w_in[lp, :, 1536:1544].rearrange("(c p) n -> p c n", p=128), dsem="wfd")

        def slab_parts(lp, s_):
            parts = []
            if s_ < 5:
                c0 = [0, 512, 1024, 1544, 2056][s_]
                parts.append((0, 4096, ("p (c n) -> p c n", dict(c=8)),
                              w_in[lp, :, c0:c0 + 512].rearrange("(c p) n -> p c n", p=128)))
            elif s_ < 7:
                h_ = s_ - SL_O0
                parts.append((0, 4096, ("p (c n) -> p c n", dict(c=8)),
                              w_o[lp, :, h_ * 512:(h_ + 1) * 512].rearrange("(c p) n -> p c n", p=128)))
            elif s_ < 15:
                f_ = s_ - SL_W1
                parts.append((0, 4096, ("p (c n) -> p c n", dict(c=8)),
                              w1[lp, :, f_ * 512:(f_ + 1) * 512].rearrange("(c p) n -> p c n", p=128)))
            else:
                dc = s_ - SL_W2
                for qd in range(4):
                    parts.append((qd * 1024, 1024, ("p (f n) -> p f n", dict(f=8)),
                                  w2[lp, qd * 1024:(qd + 1) * 1024, dc * 128:(dc + 1) * 128].rearrange("(f p) n -> p f n", p=128)))
            return parts

        GROUP_OF = lambda s_: s_ if s_ < 5 else (5 if s_ < 7 else (6 + (s_ - 7) // 4 if s_ < 15 else 8 + (s_ - 15) // 4))

        def prepass(lp):
            for s_ in range(NSLAB):
                for off, ln, (rs_, kw), src in slab_parts(lp, s_):
                    A("pool", "dma_start", [], [("wsc", lp, GROUP_OF(s_))],
                      out=wsc[lp, s_, :, off:off + ln].rearrange(rs_, **kw), in_=src, dsem=f"pp{lp}_{GROUP_OF(s_)}",
                      group=("pp", lp))

        slab_ctr = [0]
        preloaded = {}

        def preload_slab(lp, i, s_):
            k = slab_ctr[0] % NSLOT
            slab_ctr[0] += 1
            if lp == 0 and i == 0:
                for off, ln, (rs_, kw), src in slab_parts(0, s_):
                    A("pool", "dma_start", [], [("slot", k)], out=slots[k][:, off:off + ln].rearrange(rs_, **kw), in_=src,
                      dsem=f"slot{k}", group=("fill", slab_ctr[0]))
                A("sp", "dma_start", [("slot", k)], [("wsc", 0, GROUP_OF(s_))], out=wsc[0, s_], in_=slots[k][:],
                  dsem=f"pp0_{GROUP_OF(s_)}")
            else:
                A("sp", "dma_start", [("wsc", lp, GROUP_OF(s_))], [("slot", k)], out=slots[k][:], in_=wsc[lp, s_],
                  dsem=f"slot{k}")
            preloaded[(lp, i, s_)] = k

        def load_slab(lp, s_):
            key = (lp, cur_tile[0], s_)
            if key not in preloaded:
                preload_slab(lp, cur_tile[0], s_)
            return preloaded.pop(key)

        cur_tile = [0]

        def V(lp, off, n):
            return vecs[:, lp * NVL + off: lp * NVL + off + n]

        def rms_sq_chunk(c, b, xt, XK):
            A("act", "activation", [XK(c)], [("TR", 24 + c)], out=sqb[:, c * T:(c + 1) * T], in_=xt[:, c * T:(c + 1) * T],
              func=AF.Square)
            MM(ps[b][:], ones_b[:], sqb[:, c * T:(c + 1) * T], c == 0, c == 7, [("TR", 24 + c), "ones_b"], PSr(b))

        def rms_finish(lp, goff, b, xt, XK):
            k = rs_pos[0] % 2
            rs_pos[0] += 1
            A("act", "activation", PSr(b) + ["epsc"], [("rs", k)], out=rs_t[k][:], in_=ps[b][:], func=AF.Ln,
              bias=epsc[:, 0:1], scale=1.0 / D)
            A("act", "activation", [("rs", k)], [("rs", k)], out=rs_t[k][:], in_=rs_t[k][:], func=AF.Exp, scale=-0.5)
            for c in range(8):
                A("dve", "scalar_tensor_tensor", [XK(c), ("rs", k), "vecs"], [("uT", c)],
                  out=uT[:, c * T:(c + 1) * T], in0=xt[:, c * T:(c + 1) * T], scalar=V(lp, goff + c, 1),
                  in1=rs_t[k][:], op0=ALU.mult, op1=ALU.mult)

        def prefetch_dma(lp, i, par):
            xt = X2[par]
            XK = lambda c: ("xT", par, c)
            if lp == 0 and in_tok:
                for r in range(2):
                    A("sp", "dma_start", [], [("stg", r)], out=stg[r][:], in_=x_in[i * T + r * 128: i * T + (r + 1) * 128, :],
                      dsem=f"xin{r}")
            elif lp == 0:
                for c in range(8):
                    A("sp", "dma_start", [], [XK(c)], out=xt[:, c * T:(c + 1) * T], in_=x_in[i, :, c * T:(c + 1) * T],
                      dsem=f"xf{c}")
            else:
                for hf in range(2):
                    cs = range(hf * 4, hf * 4 + 4)
                    A("sp", "dma_start", [("xmid", lp - 1, i, c) for c in cs], [XK(c) for c in cs],
                      out=xt[:, hf * 4 * T:(hf + 1) * 4 * T], in_=xmid[lp - 1, i, :, hf * 4 * T:(hf + 1) * 4 * T],
                      dsem=f"xf{hf * 4}")

        def prefetch_stages(lp, i, par):
            xt = X2[par]
            XK = lambda c: ("xT", par, c)
            stages = []
            if lp == 0 and in_tok:
                def tr_stage(r):
                    k = r % 2
                    for hf in range(2):
                        b = next_bank()
                        for cq in range(4):
                            c = hf * 4 + cq
                            S.add("pe", (lambda e, b=b, cq=cq, c=c, k=k: e.transpose(
                                ps[b][:, cq * 128:(cq + 1) * 128], stg[k][:, c * 128:(c + 1) * 128], ident_f[:])),
                                [("stg", k), "ident_f"], PSr(b))
                        A("dve", "tensor_copy", PSr(b), [XK(hf * 4 + cq) for cq in range(4)],
                          out=bass.AP(xt, hf * 4 * T + r * 128, [[8 * T, 128], [T, 4], [1, 128]]),
                          in_=bass.AP(ps[b], 0, [[T, 128], [128, 4], [1, 128]]))
                    if r + 2 < 4:
                        A("act", "dma_start", [], [("stg", k)], out=stg[k][:],
                          in_=x_in[i * T + (r + 2) * 128: i * T + (r + 3) * 128, :], dsem=f"xin{k}")
                for r in range(4):
                    stages.append(lambda r=r: tr_stage(r))
            sb_ = {}

            def sq_pair(c0):
                for c in (c0, c0 + 1):
                    A("act", "activation", [XK(c)], [("sq2", c % 2)], out=sq2[c % 2][:], in_=xt[:, c * T:(c + 1) * T],
                      func=AF.Square)

            def mm_pair(c0):
                if c0 == 0:
                    sb_["b"] = next_stat()
                b = sb_["b"]
                for c in (c0, c0 + 1):
                    MM(ps[b][:], ones_b[:], sq2[c % 2][:], c == 0, c == 7, [("sq2", c % 2), "ones_b"], PSr(b))

            def st_a():
                sq_pair(0)

            def st_mid(c0):
                mm_pair(c0)
                sq_pair(c0 + 2)

            def st_end():
                mm_pair(6)
                rms_finish(lp, 0, sb_["b"], xt, XK)
            if stages:
                last_tr = stages.pop()
                stages.append(lambda: (last_tr(), st_a()))
            else:
                stages.append(st_a)
            stages.extend([lambda: st_mid(0), lambda: st_mid(2), lambda: st_mid(4), st_end])
            return stages

        U_ALL = [("uT", c) for c in range(8)]

        def proj_chunk(slot, col0):
            b = next_bank()
            for c in range(8):
                MM(ps[b][:], slots[slot][:, c * 512 + col0: c * 512 + col0 + 128], uT[:, c * T:(c + 1) * T],
                   c == 0, c == 7, [("slot", slot), ("uT", c)], PSr(b))
            return b

        pending_store = []
        prefetch_dma(0, 0, 0)
        for st_ in prefetch_stages(0, 0, 0):
            st_()
        for lp in range(nl):
            A("dve", "memset", [], ["carry"], carry[:], 0.0)
            A("dve", "memset", [], [("hin", cc) for cc in range(4)], hin[:], 0.0)
            last_layer = (lp == nl - 1)
            for i in range(NT):
                t0 = i * T
                cur_tile[0] = i
                par = (lp * NT + i) % 2
                xT = X2[par]
                XK = (lambda par: (lambda c: ("xT", par, c)))(par)
                if i + 1 < NT:
                    nxt = (lp, i + 1)
                elif lp + 1 < nl:
                    nxt = (lp + 1, 0)
                else:
                    nxt = None
                if i == 1 and lp + 1 < nl:
                    prepass(lp + 1)

                def qk_post(kind, p, b, k2):
                    sbk = next_stat()
                    MM(ps[sbk][:], blk_b[:], sq2[k2][:], True, True, [("sq2", k2), "blk_b"], PSr(sbk))
                    k = rs_pos[0] % 2
                    rs_pos[0] += 1
                    if kind == 0:
                        A("act", "activation", PSr(sbk) + ["epsc"], [("rs", k)], out=rs_t[k][:], in_=ps[sbk][:],
                          func=AF.Ln, bias=epsc[:, 1:2], scale=1.0)
                    else:
                        A("act", "activation", PSr(sbk) + ["epsc"], [("rs", k)], out=rs_t[k][:], in_=ps[sbk][:],
                          func=AF.Ln, bias=epsc[:, 0:1], scale=1.0 / 64.0)
                    A("act", "activation", [("rs", k)], [("rs", k)], out=rs_t[k][:], in_=rs_t[k][:], func=AF.Exp, scale=-0.5)
                    if kind == 0:
                        for e_ in range(2):
                            pr = slice(e_ * 64, (e_ + 1) * 64)
                            h_ = 2 * p + e_
                            A("dve", "scalar_tensor_tensor", PSr(b) + [("rs", k), "vecs"], [("qT", h_)],
                              out=qT[pr, h_ * T:(h_ + 1) * T], in0=ps[b][pr, :], scalar=vecs[pr, lp * NVL + 16: lp * NVL + 17],
                              in1=rs_t[k][pr, :], op0=ALU.mult, op1=ALU.mult)
                    else:
                        A("dve", "scalar_tensor_tensor", PSr(b) + [("rs", k), "vecs"], [("Kc", p, i)],
                          out=Kc[:, p * SEQ + t0: p * SEQ + t0 + T], in0=ps[b][:], scalar=V(lp, 17, 1),
                          in1=rs_t[k][:], op0=ALU.mult, op1=ALU.mult)

                pend = None
                qk_slots = [load_slab(lp, SL_Q), load_slab(lp, SL_K)]
                for kind in range(2):
                    for p in range(4):
                        b = proj_chunk(qk_slots[kind], p * 128)
                        k2 = (kind * 4 + p) % 2
                        A("act", "activation", PSr(b), [("sq2", k2)], out=sq2[k2][:], in_=ps[b][:], func=AF.Square)
                        if pend is not None:
                            qk_post(*pend)
                        pend = (kind, p, b, k2)
                qk_post(*pend)

                slot = load_slab(lp, SL_V)
                fb = next_stat()
                for r in range(4):
                    b = next_bank()
                    for c in range(8):
                        MM(ps[b][:], uT[:, c * T + r * 128: c * T + (r + 1) * 128], slots[slot][:, c * 512:(c + 1) * 512],
                           c == 0, c == 7, [("slot", slot), ("uT", c)], PSr(b))
                    blk = i * 4 + r
                    A("dve", "tensor_copy", PSr(b), [("Vc", i, r)],
                      out=bass.AP(Vc, blk * 520, [[32 * 520, 128], [65, 8], [1, 64]]),
                      in_=bass.AP(ps[b], 0, [[T, 128], [64, 8], [1, 64]]))
                    for c in range(8):
                        MM(ps[fb][:, r * 8:(r + 1) * 8], uT[:, c * T + r * 128: c * T + (r + 1) * 128],
                           wf[:, lp * 64 + c * 8: lp * 64 + (c + 1) * 8], c == 0, c == 7, [("uT", c), "wf"], PSr(fb))
                A("dve", "tensor_tensor", PSr(fb) + ["vecs"], ["zf"], out=zf[:], in0=ps[fb][:, 0:32], in1=V(lp, 154, 32),
                  op=ALU.add)
                A("dve", "scalar_tensor_tensor", ["zf"], ["azf"], out=azf[:], in0=zf[:], scalar=-1.0, in1=zf[:],
                  op0=ALU.mult, op1=ALU.max)
                A("act", "activation", ["azf"], ["ef"], out=ef[:], in_=azf[:], func=AF.Exp, scale=-1.0)
                A("act", "activation", ["ef"], ["ef"], out=ef[:], in_=ef[:], func=AF.Ln, bias=1.0, scale=1.0)
                A("dve", "tensor_scalar", ["zf"], ["mzf"], out=mzf[:], in0=zf[:], scalar1=0.0, scalar2=None, op0=ALU.min)
                A("dve", "tensor_tensor", ["mzf", "ef"], ["logf"], out=logf[:], in0=mzf[:], in1=ef[:], op=ALU.subtract)
                slot_a = load_slab(lp, SL_A)
                slot_g = load_slab(lp, SL_G)
                for cc in range(4):
                    ba = proj_chunk(slot_a, cc * 128)
                    bg = proj_chunk(slot_g, cc * 128)
                    A("act", "activation", PSr(bg), R_SIG(cc), out=sig_v(cc), in_=ps[bg][:], func=AF.Sigmoid)
                    A("dve", "tensor_tensor", PSr(ba) + R_SIG(cc), [("hin", cc)], out=hin[:, cc * 542 + 30: cc * 542 + 542],
                      in0=ps[ba][:], in1=sig_v(cc), op=ALU.mult)

                cb = next_stat()
                for r in range(4):
                    MM(ps[cb][:, r * 8:(r + 1) * 8], tri_f[:], logf[:, r * 8:(r + 1) * 8], True, r == 0,
                       ["logf", "tri_f"], PSr(cb))
                    for r2 in range(r):
                        MM(ps[cb][:, r * 8:(r + 1) * 8], ones_f[:], logf[:, r2 * 8:(r2 + 1) * 8], False, r2 == r - 1,
                           ["logf", "ones_f"], PSr(cb))
                for r in range(4):
                    MM(ps[cb][:, 32:40], ones_f[:], logf[:, r * 8:(r + 1) * 8], r == 0, r == 3, ["logf", "ones_f"], PSr(cb))
                for r in range(2):
                    MM(ps[cb][:, 40:48], ones_f[:], logf[:, r * 8:(r + 1) * 8], r == 0, r == 1, ["logf", "ones_f"], PSr(cb))
                A("dve", "tensor_tensor", PSr(cb) + ["carry"], [("Call", i)],
                  out=bass.AP(Call, i * 32, [[256, 128], [8, 4], [1, 8]]),
                  in0=bass.AP(ps[cb], 0, [[T, 128], [8, 4], [1, 8]]),
                  in1=bass.AP(carry, 0, [[8, 128], [0, 4], [1, 8]]), op=ALU.add)
                A("dve", "tensor_tensor", PSr(cb) + ["carry"], ["cmid"], out=cmid[:], in0=ps[cb][:, 40:48], in1=carry[:],
                  op=ALU.add)
                A("dve", "tensor_tensor", PSr(cb) + ["carry"], ["carry"], out=carry[:], in0=ps[cb][:, 32:40], in1=carry[:],
                  op=ALU.add)
                nj = 4 * i + 4
                A("dve", "tensor_tensor", ["cmid"] + [("Call", t) for t in range(i + 1)], ["BT"],
                  out=bass.AP(BT, 0, [[256, 128], [8, nj], [1, 8]]),
                  in0=bass.AP(cmid, 0, [[8, 128], [0, nj], [1, 8]]),
                  in1=bass.AP(Call, 0, [[256, 128], [8, nj], [1, 8]]), op=ALU.subtract)

                conv_thunks = []
                taps = [(cc, w_) for cc in range(4) for w_ in range(31)]
                tap_dg = {}
                built = [0]

                def build_upto(m):
                    while built[0] < min(m, len(taps)):
                        cc, w_ = taps[built[0]]
                        kd = dg_pos[0] % 8
                        dg_pos[0] += 1
                        tap_dg[built[0]] = kd
                        A("dve", "tensor_scalar", ["ident_b", "vecs"], [("dg", kd)], out=dg[kd][:], in0=ident_b[:],
                          scalar1=V(lp, 18 + cc * 31 + w_, 1), scalar2=None, op0=ALU.mult)
                        built[0] += 1

                def mk_conv(cc):
                    base = cc * 542

                    def tap(w_):
                        idx = cc * 31 + w_
                        build_upto(idx + 7)
                        kd = tap_dg[idx]
                        MM(ps[CONV_BANK][:], dg[kd][:], hin[:, base + w_: base + w_ + T], w_ == 0, w_ == 30,
                           [("dg", kd), ("hin", cc)], PSr(CONV_BANK))
                    for w_ in range(31):
                        conv_thunks.append(lambda w_=w_: tap(w_))

                    def fin():
                        A("dve", "tensor_scalar", PSr(CONV_BANK) + ["vecs"], R_ACC(cc), out=acc_v(cc), in0=ps[CONV_BANK][:],
                          scalar1=V(lp, 142 + cc, 1), scalar2=None, op0=ALU.add)
                        A("dve", "tensor_copy", [("hin", cc)], [("hin", cc)], out=hin[:, base: base + 30],
                          in_=hin[:, base + T: base + T + 30])
                        return 9
                    conv_thunks.append(fin)
                for cc in range(4):
                    mk_conv(cc)
                build_upto(7)

                sqd = TR[:, 0:4 * T]
                R_SQD = [("TR", k_) for k_ in range(4)]
                ln_banks = {}

                def ln_mean():
                    mb = CONV_BANK
                    ln_banks["m"] = mb
                    for cc in range(4):
                        MM(ps[mb][:], mean_f[:], acc_v(cc), cc == 0, cc == 3, R_ACC(cc) + ["mean_f"], PSr(mb))
                    return 4

                def ln_center():
                    mb = ln_banks["m"]
                    for cc in range(4):
                        A("dve", "tensor_tensor", PSr(mb) + R_ACC(cc), R_ACC(cc), out=acc_v(cc), in0=acc_v(cc), in1=ps[mb][:],
                          op=ALU.subtract)
                        A("dve", "tensor_tensor", R_ACC(cc), [("TR", cc)], out=sqd[:, cc * T:(cc + 1) * T], in0=acc_v(cc),
                          in1=acc_v(cc), op=ALU.mult)
                    return 18

                def ln_var():
                    vb = CONV_BANK
                    ln_banks["v"] = vb
                    for cc in range(4):
                        MM(ps[vb][:], mean_b[:], sqd[:, cc * T:(cc + 1) * T], cc == 0, cc == 3, [("TR", cc), "mean_b"], PSr(vb))
                    k = rs_pos[0] % 2
                    rs_pos[0] += 1
                    ln_banks["k"] = k
                    A("act", "activation", PSr(vb) + ["epsc"], [("rs", k)], out=rs_t[k][:], in_=ps[vb][:], func=AF.Ln,
                      bias=epsc[:, 0:1], scale=1.0)
                    A("act", "activation", [("rs", k)], [("rs", k)], out=rs_t[k][:], in_=rs_t[k][:], func=AF.Exp, scale=-0.5)
                    for cc in range(4):
                        A("dve", "tensor_tensor", R_ACC(cc) + [("rs", k)], R_ACC(cc), out=acc_v(cc), in0=acc_v(cc),
                          in1=rs_t[k][:], op=ALU.mult)
                    return None

                def ln_finish():
                    for cc in range(4):
                        A("act", "activation", R_ACC(cc) + ["vecs"], R_MIX(4 + cc), out=mix_v(4 + cc), in_=acc_v(cc),
                          func=AF.Silu, bias=V(lp, 150 + cc, 1), scale=V(lp, 146 + cc, 1))
                conv_thunks.extend([ln_mean, ln_center, ln_var])

                steps = []
                for p in range(4):
                    for e_ in range(2):
                        for j in range(nj):
                            c0 = 0 if j < 4 * i else (j - 4 * i) * 128
                            steps.append((p, e_, j, c0))
                LAG = 2
                ring[0] = [0, 1, 2]
                STAT[0] = [7]
                sbank = {}
                nsteps = len(steps)

                def emit_S(n):
                    p, e_, j, c0 = steps[n]
                    h = 2 * p + e_
                    b = next_bank()
                    sbank[n] = b
                    diag = j >= 4 * i
                    MM(ps[b][:, c0:T], Kc[:, p * SEQ + j * 128: p * SEQ + (j + 1) * 128], qT[:, h * T + c0: (h + 1) * T],
                       True, not diag, [("Kc", p, j // 4), ("qT", h)], PSr(b))
                    if diag:
                        MM(ps[b][:, c0:c0 + 128], ident_b[:], negmask[:], False, True, ["ident_b", "negmask"], PSr(b))
                    A("act", "activation", PSr(b) + ["BT"], [("P", n % 3)], out=Pb[n % 3][:, c0:T], in_=ps[b][:, c0:T],
                      func=AF.Exp, bias=BT[:, j * 8 + h: j * 8 + h + 1], scale=1.0)

                def emit_PV(n):
                    p, e_, j, c0 = steps[n]
                    h = 2 * p + e_
                    ob = 3 + h % 3
                    if e_ == 0:
                        lhsT = Vc[:, j * 520 + h * 65: j * 520 + h * 65 + 128]
                        out = ps[ob][:, c0:T]
                    else:
                        lhsT = Vc[:, j * 520 + (h - 1) * 65 + 1: j * 520 + (h - 1) * 65 + 129]
                        out = ps[ob][:, c0:T]
                    MM(out, lhsT, Pb[n % 3][:, c0:T], j == 0, j == nj - 1, [("P", n % 3), ("Vc", j // 4, j % 4), ("Vones",)],
                       PSr(ob))
                    if e_ == 1 and j == nj - 1:
                        emit_norm(p, n)

                deferred = []
                for q_, st_ in enumerate(pending_store):
                    deferred.append((3 + 3 * q_, st_))
                del pending_store[:]

                def emit_norm(p, n):
                    oa, ob_ = 3 + (2 * p) % 3, 3 + (2 * p + 1) % 3
                    A("dve", "tensor_copy", PSr(oa), [("Osb", 0)], out=Osb[0][0:65, :], in_=ps[oa][0:65, :])
                    A("dve", "tensor_copy", PSr(ob_), [("Osb", 1)], out=Osb[1][:], in_=ps[ob_][:])

                    def part2():
                        sbk = next_stat()
                        MM(ps[sbk][:], sel_a[:], Osb[0][:], True, False, [("Osb", 0), "sel_a"], PSr(sbk))
                        MM(ps[sbk][:], sel_b[:], Osb[1][:], False, True, [("Osb", 1), "sel_b"], PSr(sbk))
                        A("dve", "reciprocal", PSr(sbk), ["rinv"], out=rinv[:], in_=ps[sbk][:])
                        A("dve", "tensor_tensor", [("Osb", 0), "rinv"], R_MIX(p), out=mix_v(p)[0:64, :], in0=Osb[0][0:64, :],
                          in1=rinv[0:64, :], op=ALU.mult)
                        A("dve", "tensor_tensor", [("Osb", 1), "rinv"], R_MIX(p), out=mix_v(p)[64:128, :],
                          in0=Osb[1][64:128, :], in1=rinv[64:128, :], op=ALU.mult)
                    deferred.append((n + min(10, 2 * nj - 2), part2))

                per = -(-len(conv_thunks) // max(nsteps - 56, 8))
                hold = [0]
                for n in range(nsteps + LAG):
                    if n < nsteps:
                        emit_S(n)
                    if n - LAG >= 0:
                        emit_PV(n - LAG)
                    for d_ in [d_ for d_ in deferred if d_[0] <= n]:
                        deferred.remove(d_)
                        d_[1]()
                    for _ in range(per):
                        if conv_thunks and n >= hold[0]:
                            th = conv_thunks.pop(0)
                            r_ = th()
                            if r_:
                                hold[0] = n + r_
                                break
                while deferred:
                    deferred.pop(0)[1]()
                while conv_thunks:
                    conv_thunks.pop(0)()
                ln_finish()
                ring[0] = [0, 1, 2, 3, 4, 5]
                STAT[0] = [6, 7]

                n2b = next_stat()
                wo_slots = [load_slab(lp, SL_O0), load_slab(lp, SL_O0 + 1)]
                M_ORDER = [0, 1, 2, 4, 5, 6, 7, 3]

                def wo_mm(dc, b, m, first, last):
                    h_, dq = dc // 4, dc % 4
                    MM(ps[b][:], slots[wo_slots[h_]][:, m * 512 + dq * 128: m * 512 + (dq + 1) * 128], mix_v(m),
                       first, last, [("slot", wo_slots[h_])] + R_MIX(m), PSr(b))

                def wo_add(dc, b):
                    A("dve", "tensor_tensor", PSr(b) + [XK(dc)], [XK(dc)], out=xT[:, dc * T:(dc + 1) * T],
                      in0=xT[:, dc * T:(dc + 1) * T], in1=ps[b][:], op=ALU.add)

                def wo_post(dc):
                    A("dve", "tensor_scalar", [XK(dc), "vecs"], [("uT", dc)], out=uT[:, dc * T:(dc + 1) * T],
                      in0=xT[:, dc * T:(dc + 1) * T], scalar1=V(lp, 8 + dc, 1), scalar2=None, op0=ALU.mult)
                    A("act", "activation", [XK(dc)], [("TR", 24 + dc)], out=sqb[:, dc * T:(dc + 1) * T],
                      in_=xT[:, dc * T:(dc + 1) * T], func=AF.Square)

                def wo_fin(dc, b):
                    wo_add(dc, b)
                    wo_post(dc)

                wo_banks = {}
                for dc in range(6):
                    wo_banks[dc] = next_bank()
                    for mi, m in enumerate(M_ORDER[:7]):
                        wo_mm(dc, wo_banks[dc], m, mi == 0, False)
                for dc in range(6):
                    wo_mm(dc, wo_banks[dc], 3, False, True)
                for dc in range(3):
                    wo_add(dc, wo_banks[dc])
                for dc in range(3):
                    wo_post(dc)
                for dc in range(3, 6):
                    wo_add(dc, wo_banks[dc])
                for dc in range(3, 6):
                    wo_post(dc)
                for dc in range(6, 8):
                    b = next_bank()
                    for mi, m in enumerate(M_ORDER):
                        wo_mm(dc, b, m, mi == 0, mi == 7)
                    wo_fin(dc, b)

                k2_ = rs_pos[0] % 2
                rs_pos[0] += 1

                def norm2_stats():
                    for c in range(8):
                        MM(ps[n2b][:], ones_b[:], sqb[:, c * T:(c + 1) * T], c == 0, c == 7, [("TR", 24 + c), "ones_b"],
                           PSr(n2b))
                    A("act", "activation", PSr(n2b) + ["epsc"], [("rs", k2_)], out=rs_t[k2_][:], in_=ps[n2b][:], func=AF.Ln,
                      bias=epsc[:, 0:1], scale=1.0 / D)
                    A("act", "activation", [("rs", k2_)], [("rs", k2_)], out=rs_t[k2_][:], in_=rs_t[k2_][:], func=AF.Exp,
                      scale=-1.0)
                if nxt is not None:
                    prefetch_dma(nxt[0], nxt[1], 1 - par)
                for s_ in range(8):
                    slot = load_slab(lp, SL_W1 + s_)
                    if s_ == 0:
                        w1b = [next_bank() for _ in range(4)]
                        for fq in range(4):
                            for c in range(7):
                                MM(ps[w1b[fq]][:], slots[slot][:, c * 512 + fq * 128: c * 512 + (fq + 1) * 128],
                                   uT[:, c * T:(c + 1) * T], c == 0, False, [("slot", slot), ("uT", c)], PSr(w1b[fq]))
                        for fq in range(4):
                            MM(ps[w1b[fq]][:], slots[slot][:, 7 * 512 + fq * 128: 7 * 512 + (fq + 1) * 128],
                               uT[:, 7 * T:8 * T], False, True, [("slot", slot), ("uT", 7)], PSr(w1b[fq]))
                    for fq in range(4):
                        fc = s_ * 4 + fq
                        b = w1b[fq] if s_ == 0 else proj_chunk(slot, fq * 128)
                        A("act", "activation", PSr(b), R_HT(fc), out=hT(fc), in_=ps[b][:], func=AF.Relu)
                        A("dve", "tensor_tensor", R_HT(fc), R_HT(fc), out=hT(fc), in0=hT(fc), in1=hT(fc), op=ALU.mult)
                    if s_ == 0:
                        norm2_stats()
                pf = prefetch_stages(nxt[0], nxt[1], 1 - par) if nxt is not None else []
                for dc in range(8):
                    if pf:
                        pf.pop(0)()
                    slot = load_slab(lp, SL_W2 + dc)
                    b = next_bank()
                    for fc in range(32):
                        MM(ps[b][:], slots[slot][:, fc * 128:(fc + 1) * 128], hT(fc), fc == 0, fc == 31,
                           [("slot", slot)] + R_HT(fc), PSr(b))
                    A("dve", "tensor_tensor", PSr(b) + [("rs", k2_)], ["rinv"], out=rinv[:], in0=ps[b][:], in1=rs_t[k2_][:],
                      op=ALU.mult)
                    A("dve", "tensor_tensor", ["rinv", XK(dc)], [XK(dc)], out=xT[:, dc * T:(dc + 1) * T],
                      in0=xT[:, dc * T:(dc + 1) * T], in1=rinv[:], op=ALU.add)
                    if not last_layer and dc % 4 == 3:
                        cs = range(dc - 3, dc + 1)
                        A("act", "dma_start", [XK(c) for c in cs], [("xmid", lp, i, c) for c in cs],
                          out=xmid[lp, i, :, (dc - 3) * T:(dc + 1) * T], in_=xT[:, (dc - 3) * T:(dc + 1) * T],
                          dsem=f"of{dc - 3}")
                    elif not out_tok:
                        A("act", "dma_start", [XK(dc)], [("out", i, dc)], out=y_out[i, :, dc * T:(dc + 1) * T],
                          in_=xT[:, dc * T:(dc + 1) * T], dsem=f"of{dc}")

                if nxt is not None:
                    preload_slab(nxt[0], nxt[1], SL_Q)
                    preload_slab(nxt[0], nxt[1], SL_K)
                if last_layer and out_tok:
                    def store_half(r, hf, i=i, t0=t0, xT=xT, XK=XK):
                        k = r % 2
                        b = next_stat()
                        for cq in range(4):
                            c = hf * 4 + cq
                            S.add("pe", (lambda e, b=b, cq=cq, c=c, r=r, xT=xT: e.transpose(
                                ps[b][:, cq * 128:(cq + 1) * 128], xT[:, c * T + r * 128: c * T + (r + 1) * 128],
                                ident_f[:])), [XK(c), "ident_f"], PSr(b))
                        A("dve", "tensor_copy", PSr(b), [("stg", k)], out=stg[k][:, hf * 512:(hf + 1) * 512], in_=ps[b][:])
                        if hf == 1:
                            A("pool", "dma_start", [("stg", k)], [("out", i, r)],
                              out=y_out[t0 + r * 128: t0 + (r + 1) * 128, :], in_=stg[k][:], dsem=f"xout{k}")
                    for r in range(4):
                        for hf in range(2):
                            pending_store.append(lambda r=r, hf=hf, f=store_half: f(r, hf))

        while pending_store:
            pending_store.pop(0)()

        outs = [r for r in S.lw if isinstance(r, tuple) and r[0] == "out"]
        A("sp", "nop", outs, [])

        S.finalize()
        with nc.Block() as block:
            @block.tensor
            def _(e):
                S.emit("pe", e, esems, dsems)

            @block.scalar
            def _(e):
                S.emit("act", e, esems, dsems)

            @block.vector
            def _(e):
                S.emit("dve", e, esems, dsems)

            @block.gpsimd
            def _(e):
                S.emit("pool", e, esems, dsems)

            @block.sync
            def _(e):
                S.emit("sp", e, esems, dsems)
    return nc


def pack_vecs(inp, layers):
    cols = []
    for l in layers:
        g1 = inp["norm1_g"][l].reshape(8, 128).T
        g2 = inp["norm2_g"][l].reshape(8, 128).T
        gq = np.tile(inp["q_norm_g"][l], 2)[:, None]
        gk = np.tile(inp["k_norm_g"][l], 2)[:, None]
        cw = inp["conv_w"][l].reshape(31, 4, 128).transpose(2, 1, 0).reshape(128, 124)
        cb = inp["conv_b"][l].reshape(4, 128).T
        lg = inp["conv_ln_g"][l].reshape(4, 128).T
        lb = inp["conv_ln_b"][l].reshape(4, 128).T
        bf = np.broadcast_to(np.tile(inp["b_f"][l], 4)[None, :], (128, 32))
        cols += [g1, g2, gq, gk, cw, cb, lg, lb, bf]
    return np.ascontiguousarray(np.concatenate(cols, axis=1).astype(np.float32))


_PROG_CACHE = {}


def _prog(nl, in_tok, out_tok):
    key = (nl, in_tok, out_tok)
    if key not in _PROG_CACHE:
        _PROG_CACHE[key] = build_program(nl, in_tok, out_tok)
    return _PROG_CACHE[key]


FUSED = True


def kernel(**inputs):
    inp = {k: np.asarray(v) for k, v in inputs.items()}
    x = np.ascontiguousarray(inp["x"], dtype=np.float32)
    ncores = 8
    depth = inp["w_in"].shape[0]
    if FUSED:
        stages = [list(range(depth))]
    else:
        stages = [[l] for l in range(depth)]
    cur = [x[b] for b in range(ncores)]
    for si, layers in enumerate(stages):
        in_tok = (si == 0)
        out_tok = (si == len(stages) - 1)
        nc = _prog(len(layers), in_tok, out_tok)
        ws = {
            "w_in": np.ascontiguousarray(inp["w_in"][layers], dtype=np.float32),
            "w_o": np.ascontiguousarray(inp["w_o"][layers], dtype=np.float32),
            "w1": np.ascontiguousarray(inp["w_mlp_in"][layers], dtype=np.float32),
            "w2": np.ascontiguousarray(inp["w_mlp_out"][layers], dtype=np.float32),
            "vecs": pack_vecs(inp, layers),
        }
        in_maps = [dict(ws, x=np.ascontiguousarray(cur[b])) for b in range(ncores)]
        res = run_bass_kernel_spmd(nc, in_maps, core_ids=list(range(ncores)))
        cur = [res.results[b]["out"] for b in range(ncores)]
    return np.stack(cur, axis=0).astype(np.float32)
```

```python
import numpy as np
import concourse.bass as bass
import concourse.mybir as mybir
from concourse.bass_utils import run_bass_kernel_spmd

F32 = mybir.dt.float32
BF16 = mybir.dt.bfloat16
AF = mybir.ActivationFunctionType
ALU = mybir.AluOpType

SEQ = 4096
D = 1024
T = 512
NT = SEQ // T
NSLAB = 23
NSLOT = 3
EPS = 1e-6
NVL = 186
NEG = -30000.0

SL_Q, SL_K, SL_V, SL_A, SL_G, SL_O0, SL_W1, SL_W2 = 0, 1, 2, 3, 4, 5, 7, 15


class Op:
    __slots__ = ("eng", "fn", "cdeps", "ddeps", "idx", "sig", "dsem", "dcount", "sigidx", "group")


class Sched:
    ENGS = ("pe", "act", "dve", "pool", "sp")

    def __init__(self):
        self.q = {e: [] for e in self.ENGS}
        self.lw = {}
        self.rd = {}
        self.dcnt = {}

    def add(self, eng, fn, reads=(), writes=(), dsem=None, group=None):
        op = Op()
        op.eng = eng
        op.fn = fn
        op.sig = False
        op.dsem = dsem
        op.dcount = 0
        op.group = group
        deps = []
        for r in reads:
            w = self.lw.get(r)
            if w is not None:
                deps.append(w)
        for r in writes:
            w = self.lw.get(r)
            if w is not None and not (group is not None and w.group == group):
                deps.append(w)
            rr = self.rd.get(r)
            if rr:
                deps.extend(rr.values())
        cd = {}
        dd = {}
        for d_ in deps:
            if d_.dsem is not None:
                if dd.get(d_.dsem, 0) < d_.dcount:
                    dd[d_.dsem] = d_.dcount
            else:
                if d_.eng == "pe" and eng == "pe":
                    continue
                if cd.get(d_.eng, -1) < d_.idx:
                    cd[d_.eng] = d_.idx
                    d_.sig = True
        op.cdeps = cd
        op.ddeps = dd
        op.idx = len(self.q[eng])
        self.q[eng].append(op)
        if dsem is not None:
            self.dcnt[dsem] = self.dcnt.get(dsem, 0) + 16
            op.dcount = self.dcnt[dsem]
        for r in reads:
            if r in writes:
                continue
            m = self.rd.setdefault(r, {})
            key = eng if dsem is None else ("dma", id(op))
            m[key] = op
        for r in writes:
            self.lw[r] = op
            self.rd[r] = {}
        return op

    def emit(self, eng, e, esems, dsems):
        ops = self.q[eng]
        known = {}
        for op in ops:
            for pe_, idx in op.cdeps.items():
                tgt = self.q[pe_][idx]
                val = tgt.sigidx
                if known.get(pe_, 0) < val:
                    e.wait_ge(esems[pe_], val)
                    known[pe_] = val
            for ds, cnt in op.ddeps.items():
                if known.get(ds, 0) < cnt:
                    e.wait_ge(dsems[ds], cnt)
                    known[ds] = cnt
            ins = op.fn(e)
            if op.dsem is not None:
                ins.then_inc(dsems[op.dsem], 16)
            elif op.sig:
                ins.then_inc(esems[eng], 1)

    def finalize(self):
        for eng in self.ENGS:
            n = 0
            for op in self.q[eng]:
                if op.sig and op.dsem is None:
                    n += 1
                    op.sigidx = n
                else:
                    op.sigidx = None


def build_program(nl, in_tok, out_tok):
    nc = bass.Bass("TRN2", target_bir_lowering=False)
    if in_tok:
        x_in = nc.dram_tensor("x", [SEQ, D], F32, kind="ExternalInput").ap()
    else:
        x_in = nc.dram_tensor("x", [NT, 128, 8 * T], F32, kind="ExternalInput").ap()
    w_in = nc.dram_tensor("w_in", [nl, D, 2568], F32, kind="ExternalInput").ap()
    w_o = nc.dram_tensor("w_o", [nl, D, D], F32, kind="ExternalInput").ap()
    w1 = nc.dram_tensor("w1", [nl, D, 4 * D], F32, kind="ExternalInput").ap()
    w2 = nc.dram_tensor("w2", [nl, 4 * D, D], F32, kind="ExternalInput").ap()
    vecs_d = nc.dram_tensor("vecs", [128, nl * NVL], F32, kind="ExternalInput").ap()
    if out_tok:
        y_out = nc.dram_tensor("out", [SEQ, D], F32, kind="ExternalOutput").ap()
    else:
        y_out = nc.dram_tensor("out", [NT, 128, 8 * T], F32, kind="ExternalOutput").ap()
    wsc = nc.dram_tensor("wsc", [nl, NSLAB, 128, 4096], BF16, kind="Internal").ap()
    xmid = nc.dram_tensor("xmid", [max(nl - 1, 1), NT, 128, 8 * T], F32, kind="Internal").ap()

    S = Sched()
    from contextlib import ExitStack
    with ExitStack() as es:
        def sb(name, shape, dt):
            return es.enter_context(nc.sbuf_tensor(name, shape, dt))

        def sem(name):
            return es.enter_context(nc.semaphore(name))

        Kc = sb("Kc", [128, 4 * SEQ], BF16)
        Vc = sb("Vc", [128, 32 * 520], BF16)
        Call = sb("Call", [128, 32 * 8], F32)
        BT = sb("BT", [128, 32 * 8], F32)
        carry = sb("carry", [128, 8], F32)
        cmid = sb("cmid", [128, 8], F32)
        zf = sb("zf", [128, 32], F32)
        azf = sb("azf", [128, 32], F32)
        ef = sb("ef", [128, 32], F32)
        mzf = sb("mzf", [128, 32], F32)
        logf = sb("logf", [128, 32], F32)
        X2 = [sb("xA", [128, 8 * T], F32), sb("xB", [128, 8 * T], F32)]
        stg = [sb(f"stg{k}", [128, D], F32) for k in range(2)]
        uT = sb("uT", [128, 8 * T], BF16)
        slots = [sb(f"slot{k}", [128, 4096], BF16) for k in range(NSLOT)]
        qT = sb("qT", [128, 8 * T], BF16)
        sq2 = [sb(f"sq2_{k}", [128, T], BF16) for k in range(2)]
        hin = sb("hin", [128, 4 * 542], BF16)
        dg = [sb(f"dg{k}", [128, 128], BF16) for k in range(8)]
        TR = sb("TR", [128, 16384], BF16)
        TRf = TR.bitcast(F32)
        rs_t = [sb(f"rs{k}", [128, T], F32) for k in range(2)]
        Pb = [sb(f"P{k}", [128, T], BF16) for k in range(3)]
        Osb = [sb(f"Osb{k}", [128, T], F32) for k in range(2)]
        rinv = sb("rinv", [128, T], F32)
        vecs = sb("vecs_sb", [128, nl * NVL], F32)
        wf = sb("wf", [128, nl * 64], BF16)
        ident_f = sb("ident_f", [128, 128], F32)
        ident_b = sb("ident_b", [128, 128], BF16)
        negmask = sb("negmask", [128, 128], BF16)
        tri_f = sb("tri_f", [128, 128], F32)
        ones_f = sb("ones_f", [128, 128], F32)
        ones_b = sb("ones_b", [128, 128], BF16)
        blk_b = sb("blk_b", [128, 128], BF16)
        mean_f = sb("mean_f", [128, 128], F32)
        mean_b = sb("mean_b", [128, 128], BF16)
        sel_a = sb("sel_a", [128, 128], F32)
        sel_b = sb("sel_b", [128, 128], F32)
        epsc = sb("epsc", [128, 2], F32)
        ps = [es.enter_context(nc.psum_tensor(f"ps{k}", [128, T], F32)) for k in range(8)]

        def hT(fc):
            return TR[:, fc * T:(fc + 1) * T]

        def sig_v(cc):
            return TRf[:, cc * T:(cc + 1) * T]

        def acc_v(cc):
            return TRf[:, 2048 + cc * T: 2048 + (cc + 1) * T]

        def mix_v(m):
            return TR[:, 8192 + m * T: 8192 + (m + 1) * T]

        sqb = TR[:, 12288:16384]
        R_SIG = lambda cc: [("TR", 2 * cc), ("TR", 2 * cc + 1)]
        R_ACC = lambda cc: [("TR", 8 + 2 * cc), ("TR", 8 + 2 * cc + 1)]
        R_MIX = lambda m: [("TR", 16 + m)]
        R_SQB = [("TR", 24 + c) for c in range(8)]
        R_HT = lambda fc: [("TR", fc)]

        esems = {e: sem("e_" + e) for e in Sched.ENGS}
        dsem_names = ["vec", "wfd", "xin0", "xin1", "xout0", "xout1"] + [f"xf{c}" for c in range(8)] + [f"of{c}" for c in range(8)] + \
            [f"slot{k}" for k in range(NSLOT)] + [f"pp{l}_{g}" for l in range(nl) for g in range(10)]
        dsems = {n: sem("d_" + n) for n in dsem_names}

        def A(eng, method, reads, writes, *args, **kw):
            dsem = kw.pop("dsem", None)
            group = kw.pop("group", None)
            return S.add(eng, (lambda e: getattr(e, method)(*args, **kw)), reads, writes, dsem=dsem, group=group)

        def MM(out, lhsT, rhs, start, stop, reads, writes):
            return S.add("pe", (lambda e: e.matmul(out, lhsT, rhs, start=start, stop=stop)), reads, writes)

        ring = [[0, 1, 2, 3, 4, 5]]
        CONV_BANK = 6
        ring_pos = [0]

        def next_bank():
            b = ring[0][ring_pos[0] % len(ring[0])]
            ring_pos[0] += 1
            return b
        STAT = [[6, 7]]
        stat_pos = [0]

        def next_stat():
            b = STAT[0][stat_pos[0] % len(STAT[0])]
            stat_pos[0] += 1
            return b
        rs_pos = [0]
        dg_pos = [0]

        def PSr(b):
            return [("ps", b)]

        A("pool", "memset", [], ["ident_f"], ident_f[:], 0.0)
        A("pool", "affine_select", ["ident_f"], ["ident_f"], out=ident_f[:], in_=ident_f[:], pattern=[[-1, 128]],
          compare_op=ALU.not_equal, fill=1.0, base=0, channel_multiplier=1)
        A("pool", "tensor_copy", ["ident_f"], ["ident_b"], out=ident_b[:], in_=ident_f[:])
        A("pool", "memset", [], ["negmask"], negmask[:], 0.0)
        A("pool", "affine_select", ["negmask"], ["negmask"], out=negmask[:], in_=negmask[:], pattern=[[1, 128]],
          compare_op=ALU.is_ge, fill=NEG, base=0, channel_multiplier=-1)
        A("pool", "memset", [], ["tri_f"], tri_f[:], 1.0)
        A("pool", "affine_select", ["tri_f"], ["tri_f"], out=tri_f[:], in_=tri_f[:], pattern=[[1, 128]],
          compare_op=ALU.is_ge, fill=0.0, base=0, channel_multiplier=-1)
        A("pool", "memset", [], ["ones_f"], ones_f[:], 1.0)
        A("pool", "memset", [], ["ones_b"], ones_b[:], 1.0)
        A("pool", "memset", [], ["blk_b"], blk_b[:], 0.0)
        A("pool", "memset", ["blk_b"], ["blk_b"], blk_b[0:64, 0:64], 1.0)
        A("pool", "memset", ["blk_b"], ["blk_b"], blk_b[64:128, 64:128], 1.0)
        A("pool", "memset", [], ["mean_f"], mean_f[:], 1.0 / 512.0)
        A("pool", "memset", [], ["mean_b"], mean_b[:], 1.0 / 512.0)
        A("pool", "memset", [], ["sel_a"], sel_a[:], 0.0)
        A("pool", "memset", [], ["sel_b"], sel_b[:], 0.0)
        A("pool", "affine_select", ["sel_a"], ["sel_a"], out=sel_a[:, 0:64], in_=sel_a[:, 0:64], pattern=[[0, 64]],
          compare_op=ALU.not_equal, fill=1.0, base=-64, channel_multiplier=1)
        A("pool", "affine_select", ["sel_b"], ["sel_b"], out=sel_b[:, 64:128], in_=sel_b[:, 64:128], pattern=[[0, 64]],
          compare_op=ALU.not_equal, fill=1.0, base=-63, channel_multiplier=1)
        A("pool", "memset", [], ["epsc"], epsc[:, 0:1], EPS)
        A("pool", "memset", ["epsc"], ["epsc"], epsc[:, 1:2], 64.0 * EPS)
        A("pool", "memset", [], [("Osb", 0)], Osb[0][:], 0.0)
        A("pool", "memset", [], [("Osb", 1)], Osb[1][:], 0.0)
        A("pool", "memset", [], [("qT", h_) for h_ in range(8)], qT[:], 0.0)
        A("pool", "memset", [], [("Vones",)], bass.AP(Vc, 64, [[32 * 520, 128], [65, 256], [1, 1]]), 1.0)

        A("sp", "dma_start", [], ["vecs"], out=vecs[:], in_=vecs_d, dsem="vec")
        for lp in range(nl):
            A("pool", "dma_start", [], ["wf"], out=bass.AP(wf, lp * 64, [[nl * 64, 128], [8, 8], [1, 8]]),
              in_=w_in[lp, :, 1536:1544].rearrange("(c p) n -> p c n", p=128), dsem="wfd")

        def slab_parts(lp, s_):
            parts = []
            if s_ < 5:
                c0 = [0, 512, 1024, 1544, 2056][s_]
                parts.append((0, 4096, ("p (c n) -> p c n", dict(c=8)),
                              w_in[lp, :, c0:c0 + 512].rearrange("(c p) n -> p c n", p=128)))
            elif s_ < 7:
                h_ = s_ - SL_O0
                parts.append((0, 4096, ("p (c n) -> p c n", dict(c=8)),
                              w_o[lp, :, h_ * 512:(h_ + 1) * 512].rearrange("(c p) n -> p c n", p=128)))
            elif s_ < 15:
                f_ = s_ - SL_W1
                parts.append((0, 4096, ("p (c n) -> p c n", dict(c=8)),
                              w1[lp, :, f_ * 512:(f_ + 1) * 512].rearrange("(c p) n -> p c n", p=128)))
            else:
                dc = s_ - SL_W2
                for qd in range(4):
                    parts.append((qd * 1024, 1024, ("p (f n) -> p f n", dict(f=8)),
                                  w2[lp, qd * 1024:(qd + 1) * 1024, dc * 128:(dc + 1) * 128].rearrange("(f p) n -> p f n", p=128)))
            return parts

        GROUP_OF = lambda s_: s_ if s_ < 5 else (5 if s_ < 7 else (6 + (s_ - 7) // 4 if s_ < 15 else 8 + (s_ - 15) // 4))

        def prepass(lp):
            for s_ in range(NSLAB):
                for off, ln, (rs_, kw), src in slab_parts(lp, s_):
                    A("pool", "dma_start", [], [("wsc", lp, GROUP_OF(s_))],
                      out=wsc[lp, s_, :, off:off + ln].rearrange(rs_, **kw), in_=src, dsem=f"pp{lp}_{GROUP_OF(s_)}",
                      group=("pp", lp))

        slab_ctr = [0]
        preloaded = {}

        def preload_slab(lp, i, s_):
            k = slab_ctr[0] % NSLOT
            slab_ctr[0] += 1
            if lp == 0 and i == 0:
                for off, ln, (rs_, kw), src in slab_parts(0, s_):
                    A("pool", "dma_start", [], [("slot", k)], out=slots[k][:, off:off + ln].rearrange(rs_, **kw), in_=src,
                      dsem=f"slot{k}", group=("fill", slab_ctr[0]))
                A("sp", "dma_start", [("slot", k)], [("wsc", 0, GROUP_OF(s_))], out=wsc[0, s_], in_=slots[k][:],
                  dsem=f"pp0_{GROUP_OF(s_)}")
            else:
                A("sp", "dma_start", [("wsc", lp, GROUP_OF(s_))], [("slot", k)], out=slots[k][:], in_=wsc[lp, s_],
                  dsem=f"slot{k}")
            preloaded[(lp, i, s_)] = k

        def load_slab(lp, s_):
            key = (lp, cur_tile[0], s_)
            if key not in preloaded:
                preload_slab(lp, cur_tile[0], s_)
            return preloaded.pop(key)

        cur_tile = [0]

        def V(lp, off, n):
            return vecs[:, lp * NVL + off: lp * NVL + off + n]

        def rms_sq_chunk(c, b, xt, XK):
            A("act", "activation", [XK(c)], [("TR", 24 + c)], out=sqb[:, c * T:(c + 1) * T], in_=xt[:, c * T:(c + 1) * T],
              func=AF.Square)
            MM(ps[b][:], ones_b[:], sqb[:, c * T:(c + 1) * T], c == 0, c == 7, [("TR", 24 + c), "ones_b"], PSr(b))

        def rms_finish(lp, goff, b, xt, XK):
            k = rs_pos[0] % 2
            rs_pos[0] += 1
            A("act", "activation", PSr(b) + ["epsc"], [("rs", k)], out=rs_t[k][:], in_=ps[b][:], func=AF.Ln,
              bias=epsc[:, 0:1], scale=1.0 / D)
            A("act", "activation", [("rs", k)], [("rs", k)], out=rs_t[k][:], in_=rs_t[k][:], func=AF.Exp, scale=-0.5)
            for c in range(8):
                A("dve", "scalar_tensor_tensor", [XK(c), ("rs", k), "vecs"], [("uT", c)],
                  out=uT[:, c * T:(c + 1) * T], in0=xt[:, c * T:(c + 1) * T], scalar=V(lp, goff + c, 1),
                  in1=rs_t[k][:], op0=ALU.mult, op1=ALU.mult)

        def prefetch_dma(lp, i, par):
            xt = X2[par]
            XK = lambda c: ("xT", par, c)
            if lp == 0 and in_tok:
                for r in range(2):
                    A("sp", "dma_start", [], [("stg", r)], out=stg[r][:], in_=x_in[i * T + r * 128: i * T + (r + 1) * 128, :],
                      dsem=f"xin{r}")
            elif lp == 0:
                for c in range(8):
                    A("sp", "dma_start", [], [XK(c)], out=xt[:, c * T:(c + 1) * T], in_=x_in[i, :, c * T:(c + 1) * T],
                      dsem=f"xf{c}")
            else:
                for hf in range(2):
                    cs = range(hf * 4, hf * 4 + 4)
                    A("sp", "dma_start", [("xmid", lp - 1, i, c) for c in cs], [XK(c) for c in cs],
                      out=xt[:, hf * 4 * T:(hf + 1) * 4 * T], in_=xmid[lp - 1, i, :, hf * 4 * T:(hf + 1) * 4 * T],
                      dsem=f"xf{hf * 4}")

        def prefetch_stages(lp, i, par):
            xt = X2[par]
            XK = lambda c: ("xT", par, c)
            stages = []
            if lp == 0 and in_tok:
                def tr_stage(r):
                    k = r % 2
                    for hf in range(2):
                        b = next_bank()
                        for cq in range(4):
                            c = hf * 4 + cq
                            S.add("pe", (lambda e, b=b, cq=cq, c=c, k=k: e.transpose(
                                ps[b][:, cq * 128:(cq + 1) * 128], stg[k][:, c * 128:(c + 1) * 128], ident_f[:])),
                                [("stg", k), "ident_f"], PSr(b))
                        A("dve", "tensor_copy", PSr(b), [XK(hf * 4 + cq) for cq in range(4)],
                          out=bass.AP(xt, hf * 4 * T + r * 128, [[8 * T, 128], [T, 4], [1, 128]]),
                          in_=bass.AP(ps[b], 0, [[T, 128], [128, 4], [1, 128]]))
                    if r + 2 < 4:
                        A("act", "dma_start", [], [("stg", k)], out=stg[k][:],
                          in_=x_in[i * T + (r + 2) * 128: i * T + (r + 3) * 128, :], dsem=f"xin{k}")
                for r in range(4):
                    stages.append(lambda r=r: tr_stage(r))
            sb_ = {}

            def sq_pair(c0):
                for c in (c0, c0 + 1):
                    A("act", "activation", [XK(c)], [("sq2", c % 2)], out=sq2[c % 2][:], in_=xt[:, c * T:(c + 1) * T],
                      func=AF.Square)

            def mm_pair(c0):
                if c0 == 0:
                    sb_["b"] = next_stat()
                b = sb_["b"]
                for c in (c0, c0 + 1):
                    MM(ps[b][:], ones_b[:], sq2[c % 2][:], c == 0, c == 7, [("sq2", c % 2), "ones_b"], PSr(b))

            def st_a():
                sq_pair(0)

            def st_mid(c0):
                mm_pair(c0)
                sq_pair(c0 + 2)

            def st_end():
                mm_pair(6)
                rms_finish(lp, 0, sb_["b"], xt, XK)
            if stages:
                last_tr = stages.pop()
                stages.append(lambda: (last_tr(), st_a()))
            else:
                stages.append(st_a)
            stages.extend([lambda: st_mid(0), lambda: st_mid(2), lambda: st_mid(4), st_end])
            return stages

        U_ALL = [("uT", c) for c in range(8)]

        def proj_chunk(slot, col0):
            b = next_bank()
            for c in range(8):
                MM(ps[b][:], slots[slot][:, c * 512 + col0: c * 512 + col0 + 128], uT[:, c * T:(c + 1) * T],
                   c == 0, c == 7, [("slot", slot), ("uT", c)], PSr(b))
            return b

        pending_store = []
        prefetch_dma(0, 0, 0)
        for st_ in prefetch_stages(0, 0, 0):
            st_()
        for lp in range(nl):
            A("dve", "memset", [], ["carry"], carry[:], 0.0)
            A("dve", "memset", [], [("hin", cc) for cc in range(4)], hin[:], 0.0)
            last_layer = (lp == nl - 1)
            for i in range(NT):
                t0 = i * T
                cur_tile[0] = i
                par = (lp * NT + i) % 2
                xT = X2[par]
                XK = (lambda par: (lambda c: ("xT", par, c)))(par)
                if i + 1 < NT:
                    nxt = (lp, i + 1)
                elif lp + 1 < nl:
                    nxt = (lp + 1, 0)
                else:
                    nxt = None
                if i == 1 and lp + 1 < nl:
                    prepass(lp + 1)

                def qk_post(kind, p, b, k2):
                    sbk = next_stat()
                    MM(ps[sbk][:], blk_b[:], sq2[k2][:], True, True, [("sq2", k2), "blk_b"], PSr(sbk))
                    k = rs_pos[0] % 2
                    rs_pos[0] += 1
                    if kind == 0:
                        A("act", "activation", PSr(sbk) + ["epsc"], [("rs", k)], out=rs_t[k][:], in_=ps[sbk][:],
                          func=AF.Ln, bias=epsc[:, 1:2], scale=1.0)
                    else:
                        A("act", "activation", PSr(sbk) + ["epsc"], [("rs", k)], out=rs_t[k][:], in_=ps[sbk][:],
                          func=AF.Ln, bias=epsc[:, 0:1], scale=1.0 / 64.0)
                    A("act", "activation", [("rs", k)], [("rs", k)], out=rs_t[k][:], in_=rs_t[k][:], func=AF.Exp, scale=-0.5)
                    if kind == 0:
                        for e_ in range(2):
                            pr = slice(e_ * 64, (e_ + 1) * 64)
                            h_ = 2 * p + e_
                            A("dve", "scalar_tensor_tensor", PSr(b) + [("rs", k), "vecs"], [("qT", h_)],
                              out=qT[pr, h_ * T:(h_ + 1) * T], in0=ps[b][pr, :], scalar=vecs[pr, lp * NVL + 16: lp * NVL + 17],
                              in1=rs_t[k][pr, :], op0=ALU.mult, op1=ALU.mult)
                    else:
                        A("dve", "scalar_tensor_tensor", PSr(b) + [("rs", k), "vecs"], [("Kc", p, i)],
                          out=Kc[:, p * SEQ + t0: p * SEQ + t0 + T], in0=ps[b][:], scalar=V(lp, 17, 1),
                          in1=rs_t[k][:], op0=ALU.mult, op1=ALU.mult)

                pend = None
                qk_slots = [load_slab(lp, SL_Q), load_slab(lp, SL_K)]
                for kind in range(2):
                    for p in range(4):
                        b = proj_chunk(qk_slots[kind], p * 128)
                        k2 = (kind * 4 + p) % 2
                        A("act", "activation", PSr(b), [("sq2", k2)], out=sq2[k2][:], in_=ps[b][:], func=AF.Square)
                        if pend is not None:
                            qk_post(*pend)
                        pend = (kind, p, b, k2)
                qk_post(*pend)

                slot = load_slab(lp, SL_V)
                fb = next_stat()
                for r in range(4):
                    b = next_bank()
                    for c in range(8):
                        MM(ps[b][:], uT[:, c * T + r * 128: c * T + (r + 1) * 128], slots[slot][:, c * 512:(c + 1) * 512],
                           c == 0, c == 7, [("slot", slot), ("uT", c)], PSr(b))
                    blk = i * 4 + r
                    A("dve", "tensor_copy", PSr(b), [("Vc", i, r)],
                      out=bass.AP(Vc, blk * 520, [[32 * 520, 128], [65, 8], [1, 64]]),
                      in_=bass.AP(ps[b], 0, [[T, 128], [64, 8], [1, 64]]))
                    for c in range(8):
                        MM(ps[fb][:, r * 8:(r + 1) * 8], uT[:, c * T + r * 128: c * T + (r + 1) * 128],
                           wf[:, lp * 64 + c * 8: lp * 64 + (c + 1) * 8], c == 0, c == 7, [("uT", c), "wf"], PSr(fb))
                A("dve", "tensor_tensor", PSr(fb) + ["vecs"], ["zf"], out=zf[:], in0=ps[fb][:, 0:32], in1=V(lp, 154, 32),
                  op=ALU.add)
                A("dve", "scalar_tensor_tensor", ["zf"], ["azf"], out=azf[:], in0=zf[:], scalar=-1.0, in1=zf[:],
                  op0=ALU.mult, op1=ALU.max)
                A("act", "activation", ["azf"], ["ef"], out=ef[:], in_=azf[:], func=AF.Exp, scale=-1.0)
                A("act", "activation", ["ef"], ["ef"], out=ef[:], in_=ef[:], func=AF.Ln, bias=1.0, scale=1.0)
                A("dve", "tensor_scalar", ["zf"], ["mzf"], out=mzf[:], in0=zf[:], scalar1=0.0, scalar2=None, op0=ALU.min)
                A("dve", "tensor_tensor", ["mzf", "ef"], ["logf"], out=logf[:], in0=mzf[:], in1=ef[:], op=ALU.subtract)
                slot_a = load_slab(lp, SL_A)
                slot_g = load_slab(lp, SL_G)
                for cc in range(4):
                    ba = proj_chunk(slot_a, cc * 128)
                    bg = proj_chunk(slot_g, cc * 128)
                    A("act", "activation", PSr(bg), R_SIG(cc), out=sig_v(cc), in_=ps[bg][:], func=AF.Sigmoid)
                    A("dve", "tensor_tensor", PSr(ba) + R_SIG(cc), [("hin", cc)], out=hin[:, cc * 542 + 30: cc * 542 + 542],
                      in0=ps[ba][:], in1=sig_v(cc), op=ALU.mult)

                A("act", "activation", ["ef"], ["azf"], out=azf[:], in_=ef[:], func=AF.Exp, scale=-1.0)
                cb = next_stat()
                for r in range(4):
                    MM(ps[cb][:, r * 8:(r + 1) * 8], tri_f[:], logf[:, r * 8:(r + 1) * 8], True, r == 0,
                       ["logf", "tri_f"], PSr(cb))
                    for r2 in range(r):
                        MM(ps[cb][:, r * 8:(r + 1) * 8], ones_f[:], logf[:, r2 * 8:(r2 + 1) * 8], False, r2 == r - 1,
                           ["logf", "ones_f"], PSr(cb))
                for r in range(4):
                    MM(ps[cb][:, 32:40], ones_f[:], logf[:, r * 8:(r + 1) * 8], r == 0, r == 3, ["logf", "ones_f"], PSr(cb))
                for r in range(2):
                    MM(ps[cb][:, 40:48], ones_f[:], logf[:, r * 8:(r + 1) * 8], r == 0, r == 1, ["logf", "ones_f"], PSr(cb))
                A("dve", "tensor_tensor", PSr(cb) + ["carry"], [("Call", i)],
                  out=bass.AP(Call, i * 32, [[256, 128], [8, 4], [1, 8]]),
                  in0=bass.AP(ps[cb], 0, [[T, 128], [8, 4], [1, 8]]),
                  in1=bass.AP(carry, 0, [[8, 128], [0, 4], [1, 8]]), op=ALU.add)
                A("dve", "tensor_tensor", PSr(cb) + ["carry"], ["cmid"], out=cmid[:], in0=ps[cb][:, 40:48], in1=carry[:],
                  op=ALU.add)
                A("dve", "tensor_tensor", PSr(cb) + ["carry"], ["carry"], out=carry[:], in0=ps[cb][:, 32:40], in1=carry[:],
                  op=ALU.add)
                nj = 4 * i + 4
                A("dve", "tensor_tensor", ["cmid"] + [("Call", t) for t in range(i + 1)], ["BT"],
                  out=bass.AP(BT, 0, [[256, 128], [8, nj], [1, 8]]),
                  in0=bass.AP(cmid, 0, [[8, 128], [0, nj], [1, 8]]),
                  in1=bass.AP(Call, 0, [[256, 128], [8, nj], [1, 8]]), op=ALU.subtract)

                conv_thunks = []
                taps = [(cc, w_) for cc in range(4) for w_ in range(31)]
                tap_dg = {}
                built = [0]

                def build_upto(m):
                    while built[0] < min(m, len(taps)):
                        cc, w_ = taps[built[0]]
                        kd = dg_pos[0] % 8
                        dg_pos[0] += 1
                        tap_dg[built[0]] = kd
                        A("dve", "tensor_scalar", ["ident_b", "vecs"], [("dg", kd)], out=dg[kd][:], in0=ident_b[:],
                          scalar1=V(lp, 18 + cc * 31 + w_, 1), scalar2=None, op0=ALU.mult)
                        built[0] += 1

                def mk_conv(cc):
                    base = cc * 542

                    def tap(w_):
                        idx = cc * 31 + w_
                        build_upto(idx + 7)
                        kd = tap_dg[idx]
                        MM(ps[CONV_BANK][:], dg[kd][:], hin[:, base + w_: base + w_ + T], w_ == 0, w_ == 30,
                           [("dg", kd), ("hin", cc)], PSr(CONV_BANK))
                    for w_ in range(31):
                        conv_thunks.append(lambda w_=w_: tap(w_))

                    def fin():
                        A("dve", "tensor_scalar", PSr(CONV_BANK) + ["vecs"], R_ACC(cc), out=acc_v(cc), in0=ps[CONV_BANK][:],
                          scalar1=V(lp, 142 + cc, 1), scalar2=None, op0=ALU.add)
                        A("dve", "tensor_copy", [("hin", cc)], [("hin", cc)], out=hin[:, base: base + 30],
                          in_=hin[:, base + T: base + T + 30])
                        return 9
                    conv_thunks.append(fin)
                for cc in range(4):
                    mk_conv(cc)
                build_upto(7)

                sqd = TR[:, 0:4 * T]
                R_SQD = [("TR", k_) for k_ in range(4)]
                ln_banks = {}

                def ln_mean():
                    mb = CONV_BANK
                    ln_banks["m"] = mb
                    for cc in range(4):
                        MM(ps[mb][:], mean_f[:], acc_v(cc), cc == 0, cc == 3, R_ACC(cc) + ["mean_f"], PSr(mb))
                    return 4

                def ln_center():
                    mb = ln_banks["m"]
                    for cc in range(4):
                        A("dve", "tensor_tensor", PSr(mb) + R_ACC(cc), R_ACC(cc), out=acc_v(cc), in0=acc_v(cc), in1=ps[mb][:],
                          op=ALU.subtract)
                        A("dve", "tensor_tensor", R_ACC(cc), [("TR", cc)], out=sqd[:, cc * T:(cc + 1) * T], in0=acc_v(cc),
                          in1=acc_v(cc), op=ALU.mult)
                    return 18

                def ln_var():
                    vb = CONV_BANK
                    ln_banks["v"] = vb
                    for cc in range(4):
                        MM(ps[vb][:], mean_b[:], sqd[:, cc * T:(cc + 1) * T], cc == 0, cc == 3, [("TR", cc), "mean_b"], PSr(vb))
                    k = rs_pos[0] % 2
                    rs_pos[0] += 1
                    ln_banks["k"] = k
                    A("act", "activation", PSr(vb) + ["epsc"], [("rs", k)], out=rs_t[k][:], in_=ps[vb][:], func=AF.Ln,
                      bias=epsc[:, 0:1], scale=1.0)
                    A("act", "activation", [("rs", k)], [("rs", k)], out=rs_t[k][:], in_=rs_t[k][:], func=AF.Exp, scale=-0.5)
                    for cc in range(4):
                        A("dve", "tensor_tensor", R_ACC(cc) + [("rs", k)], R_ACC(cc), out=acc_v(cc), in0=acc_v(cc),
                          in1=rs_t[k][:], op=ALU.mult)
                    return None

                def ln_finish():
                    for cc in range(4):
                        A("act", "activation", R_ACC(cc) + ["vecs"], R_MIX(4 + cc), out=mix_v(4 + cc), in_=acc_v(cc),
                          func=AF.Silu, bias=V(lp, 150 + cc, 1), scale=V(lp, 146 + cc, 1))
                conv_thunks.extend([ln_mean, ln_center, ln_var])

                steps = []
                for p in range(4):
                    for e_ in range(2):
                        for j in range(nj):
                            c0 = 0 if j < 4 * i else (j - 4 * i) * 128
                            steps.append((p, e_, j, c0))
                LAG = 2
                ring[0] = [0, 1, 2]
                STAT[0] = [7]
                sbank = {}
                nsteps = len(steps)

                def emit_S(n):
                    p, e_, j, c0 = steps[n]
                    h = 2 * p + e_
                    b = next_bank()
                    sbank[n] = b
                    diag = j >= 4 * i
                    MM(ps[b][:, c0:T], Kc[:, p * SEQ + j * 128: p * SEQ + (j + 1) * 128], qT[:, h * T + c0: (h + 1) * T],
                       True, not diag, [("Kc", p, j // 4), ("qT", h)], PSr(b))
                    if diag:
                        MM(ps[b][:, c0:c0 + 128], ident_b[:], negmask[:], False, True, ["ident_b", "negmask"], PSr(b))
                    A("act", "activation", PSr(b) + ["BT"], [("P", n % 3)], out=Pb[n % 3][:, c0:T], in_=ps[b][:, c0:T],
                      func=AF.Exp, bias=BT[:, j * 8 + h: j * 8 + h + 1], scale=1.0)

                def emit_PV(n):
                    p, e_, j, c0 = steps[n]
                    h = 2 * p + e_
                    ob = 3 + h % 3
                    if e_ == 0:
                        lhsT = Vc[:, j * 520 + h * 65: j * 520 + h * 65 + 128]
                        out = ps[ob][:, c0:T]
                    else:
                        lhsT = Vc[:, j * 520 + (h - 1) * 65 + 1: j * 520 + (h - 1) * 65 + 129]
                        out = ps[ob][:, c0:T]
                    MM(out, lhsT, Pb[n % 3][:, c0:T], j == 0, j == nj - 1, [("P", n % 3), ("Vc", j // 4, j % 4), ("Vones",)],
                       PSr(ob))
                    if e_ == 1 and j == nj - 1:
                        emit_norm(p, n)

                deferred = []
                for q_, st_ in enumerate(pending_store):
                    deferred.append((3 + 3 * q_, st_))
                del pending_store[:]

                def emit_norm(p, n):
                    oa, ob_ = 3 + (2 * p) % 3, 3 + (2 * p + 1) % 3
                    A("dve", "tensor_copy", PSr(oa), [("Osb", 0)], out=Osb[0][0:65, :], in_=ps[oa][0:65, :])
                    A("dve", "tensor_copy", PSr(ob_), [("Osb", 1)], out=Osb[1][:], in_=ps[ob_][:])

                    def part2():
                        sbk = next_stat()
                        MM(ps[sbk][:], sel_a[:], Osb[0][:], True, False, [("Osb", 0), "sel_a"], PSr(sbk))
                        MM(ps[sbk][:], sel_b[:], Osb[1][:], False, True, [("Osb", 1), "sel_b"], PSr(sbk))
                        A("dve", "reciprocal", PSr(sbk), ["rinv"], out=rinv[:], in_=ps[sbk][:])
                        A("dve", "tensor_tensor", [("Osb", 0), "rinv"], R_MIX(p), out=mix_v(p)[0:64, :], in0=Osb[0][0:64, :],
                          in1=rinv[0:64, :], op=ALU.mult)
                        A("dve", "tensor_tensor", [("Osb", 1), "rinv"], R_MIX(p), out=mix_v(p)[64:128, :],
                          in0=Osb[1][64:128, :], in1=rinv[64:128, :], op=ALU.mult)
                    deferred.append((n + min(10, 2 * nj - 2), part2))

                per = -(-len(conv_thunks) // max(nsteps - 56, 8))
                hold = [0]
                for n in range(nsteps + LAG):
                    if n < nsteps:
                        emit_S(n)
                    if n - LAG >= 0:
                        emit_PV(n - LAG)
                    for d_ in [d_ for d_ in deferred if d_[0] <= n]:
                        deferred.remove(d_)
                        d_[1]()
                    for _ in range(per):
                        if conv_thunks and n >= hold[0]:
                            th = conv_thunks.pop(0)
                            r_ = th()
                            if r_:
                                hold[0] = n + r_
                                break
                while deferred:
                    deferred.pop(0)[1]()
                while conv_thunks:
                    conv_thunks.pop(0)()
                ln_finish()
                ring[0] = [0, 1, 2, 3, 4, 5]
                STAT[0] = [6, 7]

                n2b = next_stat()
                wo_slots = [load_slab(lp, SL_O0), load_slab(lp, SL_O0 + 1)]
                M_ORDER = [0, 1, 2, 4, 5, 6, 7, 3]

                def wo_mm(dc, b, m, first, last):
                    h_, dq = dc // 4, dc % 4
                    MM(ps[b][:], slots[wo_slots[h_]][:, m * 512 + dq * 128: m * 512 + (dq + 1) * 128], mix_v(m),
                       first, last, [("slot", wo_slots[h_])] + R_MIX(m), PSr(b))

                def wo_add(dc, b):
                    A("dve", "tensor_tensor", PSr(b) + [XK(dc)], [XK(dc)], out=xT[:, dc * T:(dc + 1) * T],
                      in0=xT[:, dc * T:(dc + 1) * T], in1=ps[b][:], op=ALU.add)

                def wo_post(dc):
                    A("dve", "tensor_scalar", [XK(dc), "vecs"], [("uT", dc)], out=uT[:, dc * T:(dc + 1) * T],
                      in0=xT[:, dc * T:(dc + 1) * T], scalar1=V(lp, 8 + dc, 1), scalar2=None, op0=ALU.mult)
                    A("act", "activation", [XK(dc)], [("TR", 24 + dc)], out=sqb[:, dc * T:(dc + 1) * T],
                      in_=xT[:, dc * T:(dc + 1) * T], func=AF.Square)

                def wo_fin(dc, b):
                    wo_add(dc, b)
                    wo_post(dc)

                wo_banks = {}
                for dc in range(6):
                    wo_banks[dc] = next_bank()
                    for mi, m in enumerate(M_ORDER[:7]):
                        wo_mm(dc, wo_banks[dc], m, mi == 0, False)
                for dc in range(6):
                    wo_mm(dc, wo_banks[dc], 3, False, True)
                for dc in range(3):
                    wo_add(dc, wo_banks[dc])
                for dc in range(3):
                    wo_post(dc)
                for dc in range(3, 6):
                    wo_add(dc, wo_banks[dc])
                for dc in range(3, 6):
                    wo_post(dc)
                for dc in range(6, 8):
                    b = next_bank()
                    for mi, m in enumerate(M_ORDER):
                        wo_mm(dc, b, m, mi == 0, mi == 7)
                    wo_fin(dc, b)

                k2_ = rs_pos[0] % 2
                rs_pos[0] += 1

                def norm2_stats():
                    for c in range(8):
                        MM(ps[n2b][:], ones_b[:], sqb[:, c * T:(c + 1) * T], c == 0, c == 7, [("TR", 24 + c), "ones_b"],
                           PSr(n2b))
                    A("act", "activation", PSr(n2b) + ["epsc"], [("rs", k2_)], out=rs_t[k2_][:], in_=ps[n2b][:], func=AF.Ln,
                      bias=epsc[:, 0:1], scale=1.0 / D)
                    A("act", "activation", [("rs", k2_)], [("rs", k2_)], out=rs_t[k2_][:], in_=rs_t[k2_][:], func=AF.Exp,
                      scale=-1.0)
                if nxt is not None:
                    prefetch_dma(nxt[0], nxt[1], 1 - par)
                for s_ in range(8):
                    slot = load_slab(lp, SL_W1 + s_)
                    if s_ == 0:
                        w1b = [next_bank() for _ in range(4)]
                        for fq in range(4):
                            for c in range(7):
                                MM(ps[w1b[fq]][:], slots[slot][:, c * 512 + fq * 128: c * 512 + (fq + 1) * 128],
                                   uT[:, c * T:(c + 1) * T], c == 0, False, [("slot", slot), ("uT", c)], PSr(w1b[fq]))
                        for fq in range(4):
                            MM(ps[w1b[fq]][:], slots[slot][:, 7 * 512 + fq * 128: 7 * 512 + (fq + 1) * 128],
                               uT[:, 7 * T:8 * T], False, True, [("slot", slot), ("uT", 7)], PSr(w1b[fq]))
                    for fq in range(4):
                        fc = s_ * 4 + fq
                        b = w1b[fq] if s_ == 0 else proj_chunk(slot, fq * 128)
                        A("act", "activation", PSr(b), R_HT(fc), out=hT(fc), in_=ps[b][:], func=AF.Relu)
                        A("dve", "tensor_tensor", R_HT(fc), R_HT(fc), out=hT(fc), in0=hT(fc), in1=hT(fc), op=ALU.mult)
                    if s_ == 0:
                        norm2_stats()
                pf = prefetch_stages(nxt[0], nxt[1], 1 - par) if nxt is not None else []
                for dc in range(8):
                    if pf:
                        pf.pop(0)()
                    slot = load_slab(lp, SL_W2 + dc)
                    b = next_bank()
                    for fc in range(32):
                        MM(ps[b][:], slots[slot][:, fc * 128:(fc + 1) * 128], hT(fc), fc == 0, fc == 31,
                           [("slot", slot)] + R_HT(fc), PSr(b))
                    A("dve", "tensor_tensor", PSr(b) + [("rs", k2_)], ["rinv"], out=rinv[:], in0=ps[b][:], in1=rs_t[k2_][:],
                      op=ALU.mult)
                    A("dve", "tensor_tensor", ["rinv", XK(dc)], [XK(dc)], out=xT[:, dc * T:(dc + 1) * T],
                      in0=xT[:, dc * T:(dc + 1) * T], in1=rinv[:], op=ALU.add)
                    if not last_layer and dc % 4 == 3:
                        cs = range(dc - 3, dc + 1)
                        A("act", "dma_start", [XK(c) for c in cs], [("xmid", lp, i, c) for c in cs],
                          out=xmid[lp, i, :, (dc - 3) * T:(dc + 1) * T], in_=xT[:, (dc - 3) * T:(dc + 1) * T],
                          dsem=f"of{dc - 3}")
                    elif not out_tok:
                        A("act", "dma_start", [XK(dc)], [("out", i, dc)], out=y_out[i, :, dc * T:(dc + 1) * T],
                          in_=xT[:, dc * T:(dc + 1) * T], dsem=f"of{dc}")

                if nxt is not None:
                    preload_slab(nxt[0], nxt[1], SL_Q)
                    preload_slab(nxt[0], nxt[1], SL_K)
                if last_layer and out_tok:
                    def store_half(r, hf, i=i, t0=t0, xT=xT, XK=XK):
                        k = r % 2
                        b = next_stat()
                        for cq in range(4):
                            c = hf * 4 + cq
                            S.add("pe", (lambda e, b=b, cq=cq, c=c, r=r, xT=xT: e.transpose(
                                ps[b][:, cq * 128:(cq + 1) * 128], xT[:, c * T + r * 128: c * T + (r + 1) * 128],
                                ident_f[:])), [XK(c), "ident_f"], PSr(b))
                        A("dve", "tensor_copy", PSr(b), [("stg", k)], out=stg[k][:, hf * 512:(hf + 1) * 512], in_=ps[b][:])
                        if hf == 1:
                            A("pool", "dma_start", [("stg", k)], [("out", i, r)],
                              out=y_out[t0 + r * 128: t0 + (r + 1) * 128, :], in_=stg[k][:], dsem=f"xout{k}")
                    for r in range(4):
                        for hf in range(2):
                            pending_store.append(lambda r=r, hf=hf, f=store_half: f(r, hf))

        while pending_store:
            pending_store.pop(0)()

        outs = [r for r in S.lw if isinstance(r, tuple) and r[0] == "out"]
        A("sp", "nop", outs, [])

        S.finalize()
        with nc.Block() as block:
            @block.tensor
            def _(e):
                S.emit("pe", e, esems, dsems)

            @block.scalar
            def _(e):
                S.emit("act", e, esems, dsems)

            @block.vector
            def _(e):
                S.emit("dve", e, esems, dsems)

            @block.gpsimd
            def _(e):
                S.emit("pool", e, esems, dsems)

            @block.sync
            def _(e):
                S.emit("sp", e, esems, dsems)
    return nc


def pack_vecs(inp, layers):
    cols = []
    for l in layers:
        g1 = inp["norm1_g"][l].reshape(8, 128).T
        g2 = inp["norm2_g"][l].reshape(8, 128).T
        gq = np.tile(inp["q_norm_g"][l], 2)[:, None]
        gk = np.tile(inp["k_norm_g"][l], 2)[:, None]
        cw = inp["conv_w"][l].reshape(31, 4, 128).transpose(2, 1, 0).reshape(128, 124)
        cb = inp["conv_b"][l].reshape(4, 128).T
        lg = inp["conv_ln_g"][l].reshape(4, 128).T
        lb = inp["conv_ln_b"][l].reshape(4, 128).T
        bf = np.broadcast_to(np.tile(inp["b_f"][l], 4)[None, :], (128, 32))
        cols += [g1, g2, gq, gk, cw, cb, lg, lb, bf]
    return np.ascontiguousarray(np.concatenate(cols, axis=1).astype(np.float32))


_PROG_CACHE = {}


def _prog(nl, in_tok, out_tok):
    key = (nl, in_tok, out_tok)
    if key not in _PROG_CACHE:
        _PROG_CACHE[key] = build_program(nl, in_tok, out_tok)
    return _PROG_CACHE[key]


FUSED = True


def kernel(**inputs):
    inp = {k: np.asarray(v) for k, v in inputs.items()}
    x = np.ascontiguousarray(inp["x"], dtype=np.float32)
    ncores = 8
    depth = inp["w_in"].shape[0]
    if FUSED:
        stages = [list(range(depth))]
    else:
        stages = [[l] for l in range(depth)]
    cur = [x[b] for b in range(ncores)]
    for si, layers in enumerate(stages):
        in_tok = (si == 0)
        out_tok = (si == len(stages) - 1)
        nc = _prog(len(layers), in_tok, out_tok)
        ws = {
            "w_in": np.ascontiguousarray(inp["w_in"][layers], dtype=np.float32),
            "w_o": np.ascontiguousarray(inp["w_o"][layers], dtype=np.float32),
            "w1": np.ascontiguousarray(inp["w_mlp_in"][layers], dtype=np.float32),
            "w2": np.ascontiguousarray(inp["w_mlp_out"][layers], dtype=np.float32),
            "vecs": pack_vecs(inp, layers),
        }
        in_maps = [dict(ws, x=np.ascontiguousarray(cur[b])) for b in range(ncores)]
        res = run_bass_kernel_spmd(nc, in_maps, core_ids=list(range(ncores)))
        cur = [res.results[b]["out"] for b in range(ncores)]
    return np.stack(cur, axis=0).astype(np.float32)
```
